# Optimizing a Trainium2 kernel written in Bass

```python
import jax, jax.numpy as jnp
from jax import lax
import numpy as np

D_MODEL = 1024
BATCH = 16
SEQ = 256
DEPTH = 2
DEC_BATCH = 8
DEC_SEQ = 4096
PAST_LEN = 512

GRID_W = 64
HEAD_DIM = 64
ATTN_WIDTH = D_MODEL // 2
N_HEADS = ATTN_WIDTH // HEAD_DIM
N_KV_HEADS = 2
GQA_GROUP = N_HEADS // N_KV_HEADS
KV_WIDTH = N_KV_HEADS * HEAD_DIM
POOL_WIDTH = D_MODEL // 4
POOL_WINDOWS = (2, 4, 8, 16)
POOL_GROUP = POOL_WIDTH // len(POOL_WINDOWS)
CONV_WIDTH = D_MODEL // 4
CONV_K = 3
WINDOW = 128
BLOCK = 128
D_FF = 2816
ROPE_BASE = 10000.0
EPS = 1e-6
NEG = -1e30
N_MOD = 9
MIX_WIDTH = ATTN_WIDTH + POOL_WIDTH + CONV_WIDTH
IN_WIDTH = ATTN_WIDTH + 2 * KV_WIDTH + POOL_WIDTH + 3 * CONV_WIDTH
IN_SPLITS = (ATTN_WIDTH, ATTN_WIDTH + KV_WIDTH, ATTN_WIDTH + 2 * KV_WIDTH,
             ATTN_WIDTH + 2 * KV_WIDTH + POOL_WIDTH,
             ATTN_WIDTH + 2 * KV_WIDTH + POOL_WIDTH + CONV_WIDTH,
             ATTN_WIDTH + 2 * KV_WIDTH + POOL_WIDTH + 2 * CONV_WIDTH)

kernel_name = "hybrid_pool_conv_swa_diffusion_step"


def rms_norm(x, g):
    xf = x.astype(jnp.float32)
    y = xf * lax.rsqrt(jnp.mean(xf * xf, axis=-1, keepdims=True) + EPS)
    return (y * g.astype(jnp.float32)).astype(x.dtype)


def ada_params(cond, w_ada, b_ada):
    m = jax.nn.silu(cond) @ w_ada + b_ada
    return m.reshape(m.shape[:-1] + (N_MOD, D_MODEL))


def modulated_norm(x, mod, k, g):
    shift, scale = mod[:, None, 3 * k], mod[:, None, 3 * k + 1]
    return rms_norm(x, g) * (1 + scale) + shift


def ffn_sublayer(x, mod, k, g, w_in, w_out):
    h = modulated_norm(x, mod, k, g)
    gt, up = jnp.split(h @ w_in, 2, axis=-1)
    return x + 0.5 * mod[:, None, 3 * k + 2] * ((jax.nn.silu(gt) * up) @ w_out)


def multiscale_pool(u, pool_w, pool_scale):
    T = u.shape[1]
    uf = u.astype(jnp.float32)
    cs = jnp.concatenate([jnp.zeros_like(uf[:, :1]), jnp.cumsum(uf, axis=1)], axis=1)
    t = jnp.arange(T)
    outs = []
    for gi, w in enumerate(POOL_WINDOWS):
        lo = jnp.clip(t - w // 2, 0, T)
        hi = jnp.clip(t + w // 2, 0, T)
        sl = slice(gi * POOL_GROUP, (gi + 1) * POOL_GROUP)
        csg = cs[..., sl]
        cnt = (hi - lo).astype(jnp.float32)[None, :, None]
        pooled = (jnp.take(csg, hi, axis=1) - jnp.take(csg, lo, axis=1)) / cnt - uf[..., sl]
        outs.append(jnp.einsum('btc,cd->btd', pooled.astype(u.dtype), pool_w[gi]))
    return jnp.concatenate(outs, axis=-1) * pool_scale


def short_conv(u, bgate, cgate, conv_w):
    T = u.shape[1]
    zp = jnp.pad(cgate * u, ((0, 0), (1, 1), (0, 0)))
    conv = zp[:, 0:T] * conv_w[0] + zp[:, 1:T + 1] * conv_w[1] + zp[:, 2:T + 2] * conv_w[2]
    return bgate * conv


def axial_rope(x):
    T = x.shape[1]
    rows = T // GRID_W
    row = jnp.repeat(jnp.arange(rows), GRID_W).astype(jnp.float32)
    col = jnp.tile(jnp.arange(GRID_W), rows).astype(jnp.float32)
    half = HEAD_DIM // 2
    inv = ROPE_BASE ** (-jnp.arange(0, half, 2, dtype=jnp.float32) / half)

    def rot(xa, pos):
        ang = pos[:, None] * inv[None, :]
        cos = jnp.cos(ang)[None, :, None, :]
        sin = jnp.sin(ang)[None, :, None, :]
        x1, x2 = jnp.split(xa, 2, axis=-1)
        return jnp.concatenate([x1 * cos - x2 * sin, x1 * sin + x2 * cos], axis=-1)

    xf = x.astype(jnp.float32)
    out = jnp.concatenate([rot(xf[..., :half], row), rot(xf[..., half:], col)], axis=-1)
    return out.astype(x.dtype)


def context_attention(q, k, v, sink):
    Bn, T = q.shape[:2]
    qg = q.reshape(Bn, T, N_KV_HEADS, GQA_GROUP, HEAD_DIM)
    s = jnp.einsum('bqkgd,bskd->bkgqs', qg, k).astype(jnp.float32) * (HEAD_DIM ** -0.5)
    s_sink = jnp.broadcast_to(sink.astype(jnp.float32).reshape(1, N_KV_HEADS, GQA_GROUP, 1, 1), s.shape[:-1] + (1,))
    p = jax.nn.softmax(jnp.concatenate([s, s_sink], axis=-1), axis=-1).astype(v.dtype)
    o = jnp.einsum('bkgqs,bskd->bqkgd', p[..., :T], v)
    return o.reshape(Bn, T, ATTN_WIDTH)


def windowed_attention(q, k, v, ck, cv, sink):
    Bn, T = q.shape[:2]
    C = ck.shape[1]
    nb = T // BLOCK
    L = 3 * BLOCK
    qb = q.reshape(Bn, nb, BLOCK, N_KV_HEADS, GQA_GROUP, HEAD_DIM)

    def band(a):
        ap = jnp.pad(a, ((0, 0), (BLOCK, BLOCK), (0, 0), (0, 0))).reshape(Bn, nb + 2, BLOCK, N_KV_HEADS, HEAD_DIM)
        return jnp.concatenate([ap[:, :nb], ap[:, 1:nb + 1], ap[:, 2:]], axis=2)

    kb, vb = band(k), band(v)
    qi = jnp.arange(BLOCK)[:, None]
    sj = jnp.arange(L)[None, :]
    kpos = jnp.arange(nb)[:, None, None] * BLOCK - BLOCK + sj[None]
    valid = (jnp.abs(sj - BLOCK - qi) <= WINDOW)[None] & (kpos >= 0) & (kpos < T)
    scale = HEAD_DIM ** -0.5
    s_loc = jnp.einsum('bnqkgd,bnskd->bnkgqs', qb, kb).astype(jnp.float32) * scale
    s_loc = jnp.where(valid[None, :, None, None], s_loc, NEG)
    s_ctx = jnp.einsum('bnqkgd,bckd->bnkgqc', qb, ck).astype(jnp.float32) * scale
    s_sink = jnp.broadcast_to(sink.astype(jnp.float32).reshape(1, 1, N_KV_HEADS, GQA_GROUP, 1, 1), s_loc.shape[:-1] + (1,))
    p = jax.nn.softmax(jnp.concatenate([s_loc, s_ctx, s_sink], axis=-1), axis=-1).astype(v.dtype)
    o = (jnp.einsum('bnkgqs,bnskd->bnqkgd', p[..., :L], vb)
         + jnp.einsum('bnkgqc,bckd->bnqkgd', p[..., L:L + C], cv))
    return o.reshape(Bn, T, ATTN_WIDTH)


def trunk_layer(x, mod, norm_g, w_ffn_in, w_ffn_out, w_in, w_out, q_norm_g, k_norm_g, sink,
                pool_w, pool_scale, conv_w, ctx_k, ctx_v, latent):
    Bn, T, _ = x.shape
    x = ffn_sublayer(x, mod, 0, norm_g[0], w_ffn_in[0], w_ffn_out[0])
    h = modulated_norm(x, mod, 1, norm_g[1])
    q, k, v, u_pool, u_conv, bg, cg = jnp.split(h @ w_in, IN_SPLITS, axis=-1)
    q = rms_norm(q.reshape(Bn, T, N_HEADS, HEAD_DIM), q_norm_g)
    k = rms_norm(k.reshape(Bn, T, N_KV_HEADS, HEAD_DIM), k_norm_g)
    v = v.reshape(Bn, T, N_KV_HEADS, HEAD_DIM)
    if latent:
        attn = windowed_attention(axial_rope(q), axial_rope(k), v, ctx_k, ctx_v, sink)
    else:
        attn = context_attention(q, k, v, sink)
    pool = multiscale_pool(u_pool, pool_w, pool_scale)
    conv = short_conv(u_conv, bg, cg, conv_w)
    y = jnp.concatenate([attn, pool, conv], axis=-1) @ w_out
    x = x + mod[:, None, 5] * y
    x = ffn_sublayer(x, mod, 2, norm_g[2], w_ffn_in[1], w_ffn_out[1])
    return x, k, v


def setup_inputs(seed: int = 0) -> dict:
    key = jax.random.key(seed)
    ks = jax.random.split(key, 20)
    f32 = jnp.float32
    nrm = lambda k, s: jax.random.normal(k, s, f32)
    return {
        'x_prompt': nrm(ks[0], (BATCH, SEQ, D_MODEL)),
        'x_sample': nrm(ks[1], (DEC_BATCH, DEC_SEQ, D_MODEL)),
        'cache_k': nrm(ks[2], (DEC_BATCH, DEPTH, PAST_LEN, N_KV_HEADS, HEAD_DIM)),
        'cache_v': nrm(ks[3], (DEC_BATCH, DEPTH, PAST_LEN, N_KV_HEADS, HEAD_DIM)),
        'c': nrm(ks[4], (DEC_BATCH, D_MODEL)),
        'c_ctx': nrm(ks[5], (D_MODEL,)),
        'w_ada': nrm(ks[6], (DEPTH, D_MODEL, N_MOD * D_MODEL)) * (0.2 * D_MODEL ** -0.5),
        'b_ada': nrm(ks[7], (DEPTH, N_MOD * D_MODEL)) * 0.02,
        'norm_g': 1.0 + 0.02 * nrm(ks[8], (DEPTH, 3, D_MODEL)),
        'w_ffn_in': nrm(ks[9], (DEPTH, 2, D_MODEL, 2 * D_FF)) * D_MODEL ** -0.5,
        'w_ffn_out': nrm(ks[10], (DEPTH, 2, D_FF, D_MODEL)) * D_FF ** -0.5,
        'w_in': nrm(ks[11], (DEPTH, D_MODEL, IN_WIDTH)) * D_MODEL ** -0.5,
        'w_out': nrm(ks[12], (DEPTH, MIX_WIDTH, D_MODEL)) * MIX_WIDTH ** -0.5,
        'q_norm_g': 1.0 + 0.02 * nrm(ks[13], (DEPTH, HEAD_DIM)),
        'k_norm_g': 1.0 + 0.02 * nrm(ks[14], (DEPTH, HEAD_DIM)),
        'sink': 0.5 * nrm(ks[15], (DEPTH, N_HEADS)),
        'pool_w': nrm(ks[16], (DEPTH, len(POOL_WINDOWS), POOL_GROUP, POOL_GROUP)) * POOL_GROUP ** -0.5,
        'pool_scale': 1.0 + 0.02 * nrm(ks[17], (DEPTH, POOL_WIDTH)),
        'conv_w': nrm(ks[18], (DEPTH, CONV_K, CONV_WIDTH)) * CONV_K ** -0.5,
    }


def reference(x_prompt, x_sample, cache_k, cache_v, c, c_ctx, w_ada, b_ada, norm_g, w_ffn_in, w_ffn_out,
              w_in, w_out, q_norm_g, k_norm_g, sink, pool_w, pool_scale, conv_w):
    xp, xs = x_prompt, x_sample
    new_k, new_v = [], []
    for l in range(DEPTH):
        mod_ctx = ada_params(c_ctx[None, :], w_ada[l], b_ada[l])
        mod_lat = ada_params(c, w_ada[l], b_ada[l])
        xp, kl, vl = trunk_layer(xp, mod_ctx, norm_g[l], w_ffn_in[l], w_ffn_out[l], w_in[l], w_out[l],
                                 q_norm_g[l], k_norm_g[l], sink[l], pool_w[l], pool_scale[l], conv_w[l],
                                 None, None, False)
        new_k.append(kl)
        new_v.append(vl)
        xs, _, _ = trunk_layer(xs, mod_lat, norm_g[l], w_ffn_in[l], w_ffn_out[l], w_in[l], w_out[l],
                               q_norm_g[l], k_norm_g[l], sink[l], pool_w[l], pool_scale[l], conv_w[l],
                               cache_k[:, l], cache_v[:, l], True)
    state_k = jnp.stack(new_k, axis=1)
    state_v = jnp.stack(new_v, axis=1)
    return (xp, xs, state_k, state_v)
```

```python
import numpy as np
import ml_dtypes
import concourse.bass as bass
import concourse.mybir as mybir
from concourse.bass_utils import run_bass_kernel_spmd

F32 = mybir.dt.float32
BF16 = mybir.dt.bfloat16
AF = mybir.ActivationFunctionType
ALU = mybir.AluOpType

ENGS = ('pe', 'act', 'dve', 'pool', 'sp')
DMA_RING = {'sp': 24, 'pool': 12, 'act': 8}
STRICT_SAME_ENGINE = True


class _Op:
    __slots__ = ('eng', 'fn', 'deps', 'need_sig', 'sig', 'is_dma', 'dsem', 'dval', 'idx', 'bg')

    def __init__(self, eng, fn, is_dma):
        self.eng = eng
        self.fn = fn
        self.deps = []
        self.need_sig = False
        self.sig = 0
        self.is_dma = is_dma
        self.dsem = None
        self.dval = 0
        self.idx = 0
        self.bg = False


class Prog:
    def __init__(self, nc):
        self.nc = nc
        self.ops = {e: [] for e in ENGS}
        self.last_w = {}
        self.readers = {}
        self.dma_hist = {q: [] for q in DMA_RING}
        self.keep = set()
        self.n = 0

    def _track(self, op, reads, writes, extra):
        deps = {}
        for r in reads:
            w = self.last_w.get(r)
            if w is not None:
                deps[id(w)] = (w, True)
            if r.startswith('ps'):
                for rd in self.readers.get(r, ()):
                    if rd.eng != op.eng:
                        deps[id(rd)] = (rd, True)
        for w_ in writes:
            lw = self.last_w.get(w_)
            if lw is not None and id(lw) not in deps:
                deps[id(lw)] = (lw, False)
            for rd in self.readers.get(w_, ()):
                if id(rd) not in deps:
                    deps[id(rd)] = (rd, False)
        for e_ in extra:
            deps[id(e_)] = (e_, True)
        for d, raw in deps.values():
            if d is op:
                continue
            if d.eng == op.eng and not d.is_dma and not op.is_dma:
                if op.eng == 'pe' or (not raw and not STRICT_SAME_ENGINE):
                    continue
            op.deps.append(d)
            d.need_sig = True
        for w_ in writes:
            self.last_w[w_] = op
            self.readers[w_] = []
        for r in reads:
            if r not in writes:
                self.readers.setdefault(r, []).append(op)

    def add(self, eng, fn, reads=(), writes=(), extra=()):
        op = _Op(eng, fn, False)
        op.idx = self.n
        self.n += 1
        self._track(op, reads, writes, extra)
        self.ops[eng].append(op)
        return op

    def dma(self, q, out, in_, reads=(), writes=(), extra=(), nonc=False, bg=False):
        if nonc:
            op = _Op(q, lambda e: e.dma_start(out=out, in_=in_, allow_slow_non_contiguous=True), True)
        else:
            op = _Op(q, lambda e: e.dma_start(out=out, in_=in_), True)
        op.idx = self.n
        op.bg = bg
        if bg:
            self.keep.update(writes)
        self.n += 1
        hist = self.dma_hist[q]
        k = DMA_RING[q]
        ex = list(extra)
        if len(hist) >= k:
            ex.append(hist[len(hist) - k])
        self._track(op, reads, writes, ex)
        op.dsem = len(hist) % k
        op.dval = 16 * (len(hist) // k + 1)
        hist.append(op)
        self.ops[q].append(op)
        return op

    def barrier(self):
        lasts = []
        for e in ENGS:
            for op in reversed(self.ops[e]):
                if not op.bg:
                    lasts.append(op)
                    break
        dmas = []
        for q in DMA_RING:
            dmas += [d for d in self.dma_hist[q][-DMA_RING[q]:] if not d.bg]
        for e in ENGS:
            self.add(e, None, extra=[l for l in lasts if l.eng != e or l.is_dma] + dmas)
        self.last_w = {k: v for k, v in self.last_w.items() if k in self.keep}
        self.readers = {}

    def finish(self):
        nc = self.nc
        dmas = []
        for q in DMA_RING:
            dmas += self.dma_hist[q][-DMA_RING[q]:]
        self.add('sp', None, extra=dmas)
        for e in ENGS:
            c = 0
            for op in self.ops[e]:
                if op.need_sig and not op.is_dma:
                    c += 1
                    op.sig = c
        import contextlib
        with contextlib.ExitStack() as st:
            esem = {e: st.enter_context(nc.semaphore("s_" + e)) for e in ENGS}
            dsem = {q: [st.enter_context(nc.semaphore(f"d_{q}{i}")) for i in range(DMA_RING[q])] for q in DMA_RING}
            block = st.enter_context(nc.Block())
            ops = self.ops

            def run(e, eng):
                seen = {}
                for op in ops[e]:
                    waits = {}
                    for d in op.deps:
                        if d.is_dma:
                            key = ('d', d.eng, d.dsem)
                            val = d.dval
                        else:
                            key = ('e', d.eng)
                            val = d.sig
                        if waits.get(key, 0) < val:
                            waits[key] = val
                    for key, val in waits.items():
                        if seen.get(key, 0) >= val:
                            continue
                        seen[key] = val
                        sem = esem[key[1]] if key[0] == 'e' else dsem[key[1]][key[2]]
                        eng.wait_ge(sem, val)
                    if op.fn is None:
                        if op.need_sig:
                            eng.nop().then_inc(esem[e], 1)
                        continue
                    ins = op.fn(eng)
                    if op.is_dma:
                        ins.then_inc(dsem[e][op.dsem], 16)
                    elif op.need_sig:
                        ins.then_inc(esem[e], 1)

            @block.tensor
            def _(eng):
                run('pe', eng)

            @block.scalar
            def _(eng):
                run('act', eng)

            @block.vector
            def _(eng):
                run('dve', eng)

            @block.gpsimd
            def _(eng):
                run('pool', eng)

            @block.sync
            def _(eng):
                run('sp', eng)


D = 1024
DFF = 2816
NJ = 22
T_S = 4096
T_P = 512
NL = 2
EPS = 1e-6
NEGBIG = -30000.0
IN_W = 1792
SQ = 512
NET = 14
GRID_W = 64


class _C:
    pass


_SBN = [0]


def _sbt(nc, name, shape, dt, **kw):
    _SBN[0] += 1
    return nc.sbuf_tensor(f"{name}_{_SBN[0]}", shape, dt, align_bytes=256)


class _Stop(Exception):
    pass


def _mm_group(P, out, pairs, reads, writes, extra=()):
    n = len(pairs)

    def fn(e):
        ins = None
        for i, (l, r) in enumerate(pairs):
            ins = e.matmul(out, l, r, start=(i == 0), stop=(i == n - 1))
        return ins
    return P.add('pe', fn, reads=reads, writes=writes, extra=extra)


def _host_constants():
    ident = np.eye(128, dtype=np.float32)
    jj = np.arange(128)[:, None]
    ii = np.arange(128)[None, :]
    m_prev = np.where(jj >= ii, 0.0, NEGBIG).astype(np.float32)
    m_next = np.where(jj <= ii, 0.0, NEGBIG).astype(np.float32)
    mask = np.stack([np.tile(m_prev, (1, 4)), np.tile(m_next, (1, 4))]).astype(np.float32)
    half = 32
    inv = (10000.0 ** (-np.arange(0, half, 2, dtype=np.float32) / half)).astype(np.float32)
    t = np.arange(T_S)
    row = (t // GRID_W).astype(np.float32)
    col = (t % GRID_W).astype(np.float32)
    cos = np.ones((128, T_S + T_P), np.float32)
    sin = np.zeros((128, T_S + T_P), np.float32)
    for p in range(128):
        d = p % 64
        pos = row if d < 32 else col
        ang = (pos * inv[d % 16]).astype(np.float32)
        cos[p, :T_S] = np.cos(ang)
        sin[p, :T_S] = np.sin(ang)
    rope = np.stack([cos, sin]).astype(np.float32)
    R = np.zeros((128, 128), np.float32)
    for p in range(128):
        if (p % 32) < 16:
            R[p, p + 16] = -1.0
        else:
            R[p, p - 16] = 1.0
    rt = np.ascontiguousarray(R.T)
    bones = np.zeros((128, 128), np.float32)
    bones[:64, :64] = 1.0
    bones[64:, 64:] = 1.0
    wins = (2, 4, 8, 16)
    invc = np.zeros((4, 128, 2, 512), np.float32)

    def cnt(tt, T, w):
        lo = np.clip(tt - w // 2, 0, T)
        hi = np.clip(tt + w // 2, 0, T)
        return (hi - lo).astype(np.float32)
    for kind in range(4):
        for c in range(2):
            for h in range(2):
                w = wins[2 * c + h]
                if kind == 3:
                    tt = np.arange(512) % 256
                    cn = cnt(tt, 256, w)
                else:
                    base = {0: 0, 1: 1024, 2: T_S - 512}[kind]
                    cn = cnt(base + np.arange(512), T_S, w)
                invc[kind, h * 64:(h + 1) * 64, c, :] = (1.0 / cn)[None, :]
    return dict(c_ident=ident, c_mask=mask, c_rope=rope, c_rt=rt, c_bones=bones, c_invc=invc)


def build_program(stop_after=None):
    nc = bass.Bass("TRN2", target_bir_lowering=False)
    C = _C()
    C.nc = nc
    P = Prog(nc)
    C.P = P
    C.pb = 0
    C.ada_store = {}
    C.peng = 'dve'

    def din(name, shape):
        return nc.dram_tensor(name, list(shape), F32, kind="ExternalInput").ap()

    def dout(name, shape):
        return nc.dram_tensor(name, list(shape), F32, kind="ExternalOutput").ap()

    C.xs = din("xs", [T_S, D])
    C.xp = din("xp", [T_P, D])
    C.ck = din("ck", [NL, 512, 128])
    C.cv = din("cv", [NL, 512, 128])
    C.cond = din("cond", [2, D])
    C.w_ada = din("w_ada", [NL, D, 9 * D])
    C.b_ada = din("b_ada", [NL, 9 * D])
    C.norm_g = din("norm_g", [NL, 3, D])
    C.w_ffn_in = din("w_ffn_in", [NL, 2, D, 2 * DFF])
    C.w_ffn_out = din("w_ffn_out", [NL, 2, DFF, D])
    C.w_in = din("w_in", [NL, D, IN_W])
    C.w_out = din("w_out", [NL, D, D])
    C.q_norm_g = din("q_norm_g", [NL, 64])
    C.k_norm_g = din("k_norm_g", [NL, 64])
    C.sink = din("sink", [NL, 8])
    C.pool_w = din("pool_w", [NL, 4, 64, 64])
    C.pool_scale = din("pool_scale", [NL, 256])
    C.conv_w = din("conv_w", [NL, 3, 256])
    C.c_ident = din("c_ident", [128, 128])
    C.c_mask = din("c_mask", [2, 128, 512])
    C.c_rope = din("c_rope", [2, 128, T_S + T_P])
    C.c_rt = din("c_rt", [128, 128])
    C.c_bones = din("c_bones", [128, 128])
    C.c_invc = din("c_invc", [4, 128, 2, 512])
    C.ys = dout("ys", [T_S, D])
    C.yp = dout("yp", [T_P, D])
    C.nk = dout("nk", [2, NL, 256, 128])
    C.nv = dout("nv", [2, NL, 256, 128])
    C.wbi = nc.dram_tensor("wbi", [NL, 2, NJ, 128, 8, 256], BF16, kind="Internal").ap()
    C.wbo = nc.dram_tensor("wbo", [NL, 2, 128, NJ, D], BF16, kind="Internal").ap()
    C.wbm = nc.dram_tensor("wbm", [NL, 128, 14, 8, 128], BF16, kind="Internal").ap()
    C.wbw = nc.dram_tensor("wbw", [NL, 128, 8, D], BF16, kind="Internal").ap()
    C.modD = nc.dram_tensor("modD", [NL, 2, 9 * D], F32, kind="Internal").ap()

    import contextlib
    with contextlib.ExitStack() as st:
        def sb(name, shape, dt=F32):
            return st.enter_context(_sbt(nc, name, list(shape), dt))
        C.ps = [st.enter_context(nc.psum_tensor(f"ps{i}", [128, 512], F32)) for i in range(8)]
        C.identb = sb("identb", [128, 128], BF16)
        C.identf = sb("identf", [128, 128])
        C.maskb = sb("maskb", [128, 2, 512], BF16)
        C.bonesb = sb("bonesb", [128, 128], BF16)
        C.onesb = sb("onesb", [128, 128], BF16)
        C.rtf = sb("rtf", [128, 128])
        C.modA = sb("modA", [128, 2, 3, 8])
        C.modB = sb("modB", [128, 2, 3, 8])
        C.gqk = sb("gqk", [128, 2])
        C.RqT = sb("RqT", [128, 128], BF16)
        C.RkT = sb("RkT", [128, 128], BF16)
        C.sinkexp = sb("sinkexp", [128, 1024])
        C.cw = sb("cw", [128, 2, 3])
        C.psc = sb("psc", [128, 2])
        C.pwbd = sb("pwbd", [128, 2, 128], BF16)
        C.ckT = sb("ckT", [128, 512], BF16)
        C.cvS = sb("cvS", [128, 4, 128], BF16)
        C.gate = sb("gate", [128, D])
        C.epsc = sb("epsc", [128, 1])

        _emit_all(C, stop_after)
        P.finish()
    return nc


def _psn(C):
    i = C.pb
    C.pb = (C.pb + 1) % 8
    return i


def _emit_all(C, stop_after):
    nc, P = C.nc, C.P
    _consts(C)
    first = True
    for l in range(NL):
        _ada(C, [l])
        if l == 0:
            _conv_ffn(C, 0, 0, extra=[C.ada_store[(0, 11)]])
        P.barrier()
        if l == 0:
            _conv_mix(C, 0)
            _conv_ffn(C, 0, 1)
            _conv_ffn(C, 1, 0)
            _conv_mix(C, 1)
            _conv_ffn(C, 1, 1)
        if stop_after == 'ada':
            return
        _layer_setup(C, l)
        P.barrier()
        if stop_after == 'lsetup':
            return
        _ffn_phase(C, l, 0, 0, first)
        first = False
        P.barrier()
        if stop_after == ('ffn', l, 0):
            return
        _mixer_phase(C, l)
        P.barrier()
        if stop_after == ('mix', l):
            return
        _ffn_phase(C, l, 1, 2, False)
        P.barrier()
        if stop_after == ('ffn', l, 1):
            return


def _conv_ffn(C, l, f, extra=()):
    P = C.P
    for j in range(NJ):
        for gu in range(2):
            P.dma('pool', C.wbi[l, f, j][:, :, gu * 128:(gu + 1) * 128],
                  C.w_ffn_in[l, f][:, gu * DFF + j * 128: gu * DFF + (j + 1) * 128].rearrange(
                      "(kc p) c -> p kc c", p=128),
                  writes=[f'wbi{l}{f}_{j}'], bg=True, extra=(extra if (j == 0 and gu == 0) else ()))
    for h in range(2):
        P.dma('pool', C.wbo[l, f][:, h * 11:(h + 1) * 11, :],
              C.w_ffn_out[l, f][h * 11 * 128:(h + 1) * 11 * 128, :].rearrange("(jc p) n -> p jc n", p=128),
              writes=[f'wbo{l}{f}_{h}'], bg=True)


def _conv_mix(C, l):
    P = C.P
    for c in range(4):
        for hf in range(2):
            head = c + 4 * hf
            P.dma('pool', C.wbm[l][:, c, :, hf * 64:(hf + 1) * 64],
                  C.w_in[l][:, head * 64:(head + 1) * 64].rearrange("(kc p) c -> p kc c", p=128),
                  writes=[f'wbm{l}'], bg=True)
    for ch in range(4, 14):
        P.dma('pool', C.wbm[l][:, ch, :, :],
              C.w_in[l][:, ch * 128:(ch + 1) * 128].rearrange("(kc p) c -> p kc c", p=128), writes=[f'wbm{l}'], bg=True)
    for hf in range(2):
        P.dma('pool', C.wbw[l][hf * 64:(hf + 1) * 64, 0:4, :],
              C.w_out[l][hf * 256:(hf + 1) * 256, :].rearrange("(c p) n -> p c n", p=64), writes=[f'wbw{l}'], bg=True)
    P.dma('pool', C.wbw[l][:, 4:8, :], C.w_out[l][512:1024, :].rearrange("(c p) n -> p c n", p=128),
          writes=[f'wbw{l}'], bg=True)


def _consts(C):
    nc, P = C.nc, C.P
    P.dma('pool', C.identb[:], C.c_ident, writes=['identb'], bg=True)
    P.dma('pool', C.bonesb[:], C.c_bones, writes=['bonesb'], bg=True)
    for m in range(2):
        P.dma('pool', C.maskb[:, m, :], C.c_mask[m], writes=['maskb'], bg=True)
    P.dma('sp', C.identf[:], C.c_ident, writes=['identf'])
    P.dma('sp', C.rtf[:], C.c_rt, writes=['rtf'])
    P.add('dve', lambda e: e.memset(C.onesb[:], 1.0), writes=['onesb'])
    P.add('dve', lambda e: e.memset(C.epsc[:], EPS), writes=['epsc'])


def _ada(C, layers):
    nc, P = C.nc, C.P
    NS = 4
    with (_sbt(nc, "condT", [128, 8, 2], F32) as condT,
          _sbt(nc, "scT", [128, 8, 2], BF16) as scT,
          _sbt(nc, "wa", [128, NS, 8, 512], F32) as wa,
          _sbt(nc, "wab", [128, NS, 8, 512], BF16) as wab,
          _sbt(nc, "brow", [2, NS, 512], F32) as brow,
          _sbt(nc, "mrow", [2, NS, 512], F32) as mrow):
        for w in range(2):
            P.dma('sp', condT[:, :, w], C.cond[w].rearrange("(kc p) -> p kc", p=128), writes=[f'condT{w}'], nonc=True)
        P.add('act', lambda e: e.activation(scT[:], condT[:], AF.Silu), reads=['condT0', 'condT1'], writes=['scT'])
        items = [(l, cg) for l in layers for cg in range(18)]

        def issue(i):
            l, cg = items[i]
            s = i % NS
            P.dma('sp', wa[:, s], C.w_ada[l][:, cg * 512:(cg + 1) * 512].rearrange("(kc p) n -> p kc n", p=128),
                  writes=[f'wa{s}'])
            for w in range(2):
                P.dma('sp', brow[w:w + 1, s, :], C.b_ada[l:l + 1, cg * 512:(cg + 1) * 512], writes=[f'brow{s}_{w}'])

        for i in range(min(NS - 1, len(items))):
            issue(i)
        for i, (l, cg) in enumerate(items):
            if i + NS - 1 < len(items):
                issue(i + NS - 1)
            s = i % NS
            P.add('dve', lambda e, s=s: e.tensor_copy(wab[:, s, 0:4], wa[:, s, 0:4]), reads=[f'wa{s}'], writes=[f'wabA{s}'])
            P.add('act', lambda e, s=s: e.copy(wab[:, s, 4:8], wa[:, s, 4:8]), reads=[f'wa{s}'], writes=[f'wabB{s}'])
            pb = _psn(C)
            _mm_group(P, C.ps[pb][0:2, :], [(scT[:, kc, :], wab[:, s, kc, :]) for kc in range(8)],
                      reads=['scT', f'wabA{s}', f'wabB{s}'], writes=[f'ps{pb}'])
            P.add('dve', lambda e, s=s, pb=pb: e.tensor_tensor(mrow[:, s, :], C.ps[pb][0:2, :], brow[:, s, :], ALU.add),
                  reads=[f'ps{pb}', f'brow{s}_0', f'brow{s}_1'], writes=[f'mrow{s}'])
            C.ada_store[(l, cg)] = P.dma('sp', C.modD[l][:, cg * 512:(cg + 1) * 512], mrow[:, s, :], reads=[f'mrow{s}'],
                                         writes=[f'modD{l}_{cg}'])


def _layer_setup(C, l):
    nc, P = C.nc, C.P
    with (_sbt(nc, "sc", [128, 2, 3, 8], F32) as sc,
          _sbt(nc, "g3", [128, 3, 8], F32) as g3,
          _sbt(nc, "sk", [128, 8], F32) as sk,
          _sbt(nc, "ske", [128, 8], F32) as ske,
          _sbt(nc, "pwf", [128, 2, 128], F32) as pwf,
          _sbt(nc, "ckf", [128, 4, 128], F32) as ckf,
          _sbt(nc, "ckb", [128, 4, 128], BF16) as ckb,
          _sbt(nc, "cvf", [128, 4, 128], F32) as cvf):
        for w in range(2):
            for s in range(3):
                P.dma('sp', sc[:, w, s, :], C.modD[l, w, (3 * s + 1) * D:(3 * s + 2) * D].rearrange("(c p) -> p c", p=128),
                      reads=[f'modD{l}'], writes=[f'sc{w}{s}'], nonc=True)
                P.dma('sp', C.modB[:, w, s, :], C.modD[l, w, (3 * s) * D:(3 * s + 1) * D].rearrange("(c p) -> p c", p=128),
                      reads=[f'modD{l}'], writes=[f'modB{w}{s}'], nonc=True)
        for s in range(3):
            P.dma('sp', g3[:, s, :], C.norm_g[l, s].rearrange("(c p) -> p c", p=128), writes=[f'g3{s}'], nonc=True)
        for w in range(2):
            for s in range(3):
                P.add('dve', lambda e, w=w, s=s: e.scalar_tensor_tensor(C.modA[:, w, s, :], sc[:, w, s, :], 1.0, g3[:, s, :],
                                                                        ALU.add, ALU.mult),
                      reads=[f'sc{w}{s}', f'g3{s}'], writes=[f'modA{w}{s}'])
        for hf in range(2):
            P.dma('sp', C.gqk[hf * 64:(hf + 1) * 64, 0:1], C.q_norm_g[l].rearrange("(p o) -> p o", o=1), writes=[f'gqk{hf}0'], nonc=True)
            P.dma('sp', C.gqk[hf * 64:(hf + 1) * 64, 1:2], C.k_norm_g[l].rearrange("(p o) -> p o", o=1), writes=[f'gqk{hf}1'], nonc=True)
        P.add('dve', lambda e: e.tensor_scalar(C.RqT[:], C.rtf[:], C.gqk[:, 0:1], None, ALU.mult), reads=['rtf', 'gqk00', 'gqk10'], writes=['RqT'])
        P.add('dve', lambda e: e.tensor_scalar(C.RkT[:], C.rtf[:], C.gqk[:, 1:2], None, ALU.mult), reads=['rtf', 'gqk01', 'gqk11'], writes=['RkT'])
        P.dma('sp', sk[:], C.sink[l].partition_broadcast(128), writes=['sk'], nonc=True)
        P.add('act', lambda e: e.activation(ske[:], sk[:], AF.Exp), reads=['sk'], writes=['ske'])
        P.add('dve', lambda e: e.memset(C.sinkexp[:], 0.0), writes=['sinkexp'])
        for h in range(8):
            P.add('dve', lambda e, h=h: e.tensor_scalar(C.sinkexp[:, h * 128:(h + 1) * 128], C.sinkexp[:, h * 128:(h + 1) * 128],
                                                        ske[:, h:h + 1], None, ALU.add),
                  reads=['ske', 'sinkexp'], writes=['sinkexp'])
        for k in range(3):
            P.dma('sp', C.cw[:, :, k], C.conv_w[l, k].rearrange("(c p) -> p c", p=128), writes=[f'cw{k}'], nonc=True)
        P.dma('sp', C.psc[:], C.pool_scale[l].rearrange("(c p) -> p c", p=128), writes=['psc'], nonc=True)
        P.add('dve', lambda e: e.memset(pwf[:], 0.0), writes=['pwf'])
        for c in range(2):
            for h in range(2):
                P.dma('sp', pwf[h * 64:(h + 1) * 64, c, h * 64:(h + 1) * 64], C.pool_w[l, 2 * c + h], reads=['pwf'], writes=[f'pwf{c}{h}'])
        P.add('dve', lambda e: e.tensor_copy(C.pwbd[:], pwf[:]), reads=['pwf', 'pwf00', 'pwf01', 'pwf10', 'pwf11'], writes=['pwbd'])
        P.dma('sp', ckf[:], C.ck[l].rearrange("(kc p) f -> p kc f", p=128), writes=['ckf'])
        P.dma('sp', cvf[:], C.cv[l].rearrange("(kc p) f -> p kc f", p=128), writes=['cvf'])
        P.add('dve', lambda e: e.tensor_copy(ckb[:], ckf[:]), reads=['ckf'], writes=['ckb'])
        P.add('dve', lambda e: e.tensor_copy(C.cvS[:], cvf[:]), reads=['cvf'], writes=['cvS'])
        pb = _psn(C)
        psb = C.ps[pb][:].bitcast(BF16)

        def tr(e):
            ins = None
            for kc in range(4):
                ins = e.transpose(psb[:, kc * 128:(kc + 1) * 128], ckb[:, kc, :], C.identb[:])
            return ins
        P.add('pe', tr, reads=['ckb', 'identb'], writes=[f'ps{pb}'])
        P.add('dve', lambda e: e.tensor_copy(C.ckT[:], psb[:, 0:512]), reads=[f'ps{pb}'], writes=['ckT'])


def _load_gate(C, l, s, w):
    P = C.P
    P.dma('sp', C.gate[:, :], C.modD[l, w, (3 * s + 2) * D:(3 * s + 3) * D].partition_broadcast(128),
          reads=[f'modD{l}'], writes=['gate'], nonc=True)


def _norm_p1(C, xin_ap, ss_col, rs_col, xn_ap, tags, gb):
    P = C.P
    xin_res, xn_res, hT_res = tags
    P.add('act', lambda e: e.activation(xn_ap, xin_ap, AF.Square, accum_out=ss_col), reads=[xin_res, f'ss{gb}'], writes=[xn_res, f'ss{gb}'])
    P.add('act', lambda e: e.activation(rs_col, ss_col, AF.Sqrt, bias=C.epsc[:, 0:1], scale=1.0 / D), reads=[f'ss{gb}', 'epsc'], writes=[f'rs{gb}'])
    P.add('dve', lambda e: e.reciprocal(rs_col, rs_col), reads=[f'rs{gb}'], writes=[f'rs{gb}'])
    P.add('dve', lambda e: e.tensor_scalar(xn_ap, xin_ap, rs_col, None, ALU.mult), reads=[xin_res, f'rs{gb}', xn_res], writes=[xn_res])


def _norm_p2(C, xn_ap, hT_dst, w, s, tags, gb):
    P = C.P
    xin_res, xn_res, hT_res = tags
    pb = _psn(C)
    psb = C.ps[pb][:].bitcast(BF16)

    def tr(e):
        ins = None
        for fc in range(8):
            ins = e.transpose(psb[:, fc * 128:(fc + 1) * 128], xn_ap[:, fc * 128:(fc + 1) * 128], C.identb[:])
        return ins
    P.add('pe', tr, reads=[xn_res, 'identb'], writes=[f'ps{pb}'])
    for fc in range(8):
        P.add('dve', lambda e, fc=fc: e.tensor_scalar(hT_dst(fc), psb[:, fc * 128:(fc + 1) * 128],
                                                      C.modA[:, w, s, fc:fc + 1], C.modB[:, w, s, fc:fc + 1], ALU.mult, ALU.add),
              reads=[f'ps{pb}', 'modA', 'modB'], writes=[f'{hT_res}{fc}'])


def _norm_block(C, xin_ap, ss_col, rs_col, xn_ap, hT_dst, w, s, tags, gb):
    _norm_p1(C, xin_ap, ss_col, rs_col, xn_ap, tags, gb)
    _norm_p2(C, xn_ap, hT_dst, w, s, tags, gb)


def _ffn_phase(C, l, f, s, first):
    nc, P = C.nc, C.P
    _load_gate(C, l, s, 0)
    tiles = []
    for t in range(4):
        tiles.append((C.xs if first else C.ys, C.ys, t * 1024, 8, 0))
    tiles.append((C.xp if first else C.yp, C.yp, 0, 4, 1))
    with (_sbt(nc, "xin", [128, 3, D], F32) as xin,
          _sbt(nc, "xres", [128, 3, D], F32) as xres,
          _sbt(nc, "xn", [128, 2, D], BF16) as xn,
          _sbt(nc, "hT", [128, 8, 1024], BF16) as hT,
          _sbt(nc, "actT", [128, NJ, 1024], BF16) as actT,
          _sbt(nc, "wi", [128, 4, 8, 256], BF16) as wi,
          _sbt(nc, "wo", [128, NJ, D], BF16) as wo,
          _sbt(nc, "sil", [128, 4, 512], F32) as sil,
          _sbt(nc, "tmp", [128, 2, D], F32) as tmp,
          _sbt(nc, "ssb", [128, 40], F32) as ssb,
          _sbt(nc, "rsb", [128, 40], F32) as rsb):
        P.add('dve', lambda e: e.memset(ssb[:], 0.0), writes=[f'ss{i}' for i in range(40)])
        nsil = 0
        nwi = 0
        nres = 0
        gstart = []
        g_ = 0
        for tl in tiles:
            gstart.append(g_)
            g_ += tl[3]

        def a0_ld(k, b):
            (src_, dst_, row0_, nb_, w_) = tiles[k]
            slot = (gstart[k] + b) % 3
            P.dma('sp', xin[:, slot, :], src_[row0_ + b * 128: row0_ + (b + 1) * 128, :], reads=[f'X{w_}_{row0_ + b * 128}'],
                  writes=[f'xin{slot}'])

        def a0_args(k, b):
            (src_, dst_, row0_, nb_, w_) = tiles[k]
            gb = gstart[k] + b
            slot = gb % 3
            xs_ = gb % 2
            return gb, slot, xs_, w_

        def a0_p1(k, b):
            gb, slot, xs_, w_ = a0_args(k, b)
            _norm_p1(C, xin[:, slot, :], ssb[:, gb:gb + 1], rsb[:, gb:gb + 1], xn[:, xs_, :], (f'xin{slot}', f'xn{xs_}', 'hT'), gb)

        def a0_p2(k, b):
            gb, slot, xs_, w_ = a0_args(k, b)
            _norm_p2(C, xn[:, xs_, :], lambda fc, b=b: hT[:, fc, b * 128:(b + 1) * 128], w_, s, (f'xin{slot}', f'xn{xs_}', 'hT'), gb)

        for b in range(2):
            a0_ld(0, b)
        for b in range(tiles[0][3]):
            if b + 2 < tiles[0][3]:
                a0_ld(0, b + 2)
            a0_p1(0, b)
            a0_p2(0, b)
        for ti, (src, dst, row0, nb, w) in enumerate(tiles):
            ntok = nb * 128
            if w == 1:
                _load_gate(C, l, s, 1)
            nxt = ti + 1 if ti + 1 < len(tiles) else None
            nnb = tiles[nxt][3] if nxt is not None else 0
            def ld_wi(j):
                slot = (nwi + j) % 4
                P.dma('sp', wi[:, slot], C.wbi[l, f, j], reads=[f'wbi{l}{f}_{j}'], writes=[f'wi{slot}'])
            for j in range(3):
                ld_wi(j)
            for h in range(2):
                P.dma('sp', wo[:, h * 11:(h + 1) * 11, :], C.wbo[l, f][:, h * 11:(h + 1) * 11, :],
                      reads=[f'wbo{l}{f}_{h}'], writes=['wo'])
            nh_n = ntok // 512
            for j in range(NJ):
                if j + 3 < NJ:
                    ld_wi(j + 3)
                slot = (nwi + j) % 4
                for nh in range(nh_n):
                    pg = _psn(C)
                    pu = _psn(C)
                    tok = slice(nh * 512, (nh + 1) * 512)
                    _mm_group(P, C.ps[pg][:, :], [(wi[:, slot, kc, 0:128], hT[:, kc, tok]) for kc in range(8)],
                              reads=[f'wi{slot}'] + [f'hT{k}' for k in range(8)], writes=[f'ps{pg}'])
                    _mm_group(P, C.ps[pu][:, :], [(wi[:, slot, kc, 128:256], hT[:, kc, tok]) for kc in range(8)],
                              reads=[f'wi{slot}'] + [f'hT{k}' for k in range(8)], writes=[f'ps{pu}'])
                    ss_ = nsil % 4
                    nsil += 1
                    P.add('act', lambda e, ss_=ss_, pg=pg: e.activation(sil[:, ss_, :], C.ps[pg][:, :], AF.Silu),
                          reads=[f'ps{pg}'], writes=[f'sil{ss_}'])
                    P.add('dve', lambda e, ss_=ss_, pu=pu, j=j, tok=tok: e.tensor_tensor(actT[:, j, tok], sil[:, ss_, :], C.ps[pu][:, :], ALU.mult),
                          reads=[f'sil{ss_}', f'ps{pu}'], writes=[f'actT{j}'])
            nwi += NJ
            def ld_res(b):
                slot = (nres + b) % 3
                P.dma('sp', xres[:, slot, :], src[row0 + b * 128: row0 + (b + 1) * 128, :], reads=[f'X{w}_{row0 + b * 128}'], writes=[f'xres{slot}'])
            for b in range(min(2, nb)):
                ld_res(b)
            if nxt is not None:
                a0_ld(nxt, 0)
                a0_ld(nxt, 1)
            for b in range(nb):
                if b + 2 < nb:
                    ld_res(b + 2)
                if nxt is not None and b < nnb:
                    a0_p1(nxt, b)
                slot = (nres + b) % 3
                ts_ = (nres + b) % 2
                for hf in range(2):
                    pb = _psn(C)
                    fs = slice(hf * 512, (hf + 1) * 512)
                    _mm_group(P, C.ps[pb][:, :], [(actT[:, jc, b * 128:(b + 1) * 128], wo[:, jc, fs]) for jc in range(NJ)],
                              reads=[f'actT{jc}' for jc in range(NJ)] + ['wo'], writes=[f'ps{pb}'])
                    P.add('dve', lambda e, pb=pb, fs=fs, ts_=ts_, w=w: e.scalar_tensor_tensor(
                        tmp[:, ts_, fs], C.ps[pb][:, :], 0.5, C.gate[:, fs], ALU.mult, ALU.mult),
                        reads=[f'ps{pb}', 'gate'], writes=[f'tmp{ts_}'])
                    P.add('dve', lambda e, fs=fs, ts_=ts_, slot=slot: e.tensor_tensor(
                        xres[:, slot, fs], tmp[:, ts_, fs], xres[:, slot, fs], ALU.add),
                        reads=[f'tmp{ts_}', f'xres{slot}'], writes=[f'xres{slot}'])
                P.dma('sp', dst[row0 + b * 128: row0 + (b + 1) * 128, :], xres[:, slot, :], reads=[f'xres{slot}'],
                      writes=[f'X{w}_{row0 + b * 128}'])
                if nxt is not None and b < nnb:
                    a0_p2(nxt, b)
                    if b + 2 < nnb:
                        a0_ld(nxt, b + 2)
            nres += nb


def _mixer_phase(C, l):
    nc, P = C.nc, C.P
    C.cur_l = l
    _load_gate(C, l, 1, 0)
    import contextlib
    with contextlib.ExitStack() as st:
        def sb(name, shape, dt=F32):
            return st.enter_context(_sbt(nc, name, list(shape), dt))
        wmi = sb("wmi", [128, 14, 8, 128], BF16)
        wmo = sb("wmo", [128, 8, D], BF16)
        xin = sb("m_xin", [128, 2, D])
        xres = sb("m_xres", [128, 2, D])
        xn = sb("m_xn", [128, 2, D], BF16)
        hT = sb("m_hT", [128, 8, 512], BF16)
        qT = sb("qT", [128, 2, 4, 512], BF16)
        kT = sb("kT", [128, T_S + T_P], BF16)
        vS = sb("vS", [128, 36, 128], BF16)
        uP = sb("uP", [128, 2, 2, 528])
        zc = sb("zc", [128, 2, 2, 514])
        bgT = sb("bgT", [128, 2, 2, 512], BF16)
        mixT = sb("mixT", [128, 8, 512], BF16)
        rope = sb("rope", [128, 2, 512])
        invc = sb("invc", [128, 2, 512])
        eT = sb("eT", [128, NET, 512], BF16)
        t1r = sb("t1", [128, 2, 512])
        t2r = sb("t2", [128, 2, 512])
        rsqr = sb("rsq", [128, 2, 512])
        sqbr = sb("sqb", [128, 2, 512], BF16)
        qbr = sb("qb", [128, 2, 512], BF16)
        ucv = sb("ucv", [128, 512])
        khl = sb("khl", [128, 2, 512], BF16)
        pA = sb("pA", [128, 2, 528])
        pB = sb("pB", [128, 2, 528])
        plb = sb("plb", [128, 2, 512], BF16)
        acc = sb("acc", [128, 2, 512])
        kout = acc[:, 0, :].rearrange("p (b f) -> p b f", b=4)
        vout = acc[:, 1, :].rearrange("p (b f) -> p b f", b=4)
        tmp = sb("m_tmp", [128, 2, 512])
        dsum = sb("dsum", [128, 512])
        ssb = sb("m_ssb", [128, 40])
        rsb = sb("m_rsb", [128, 40])

        P.dma('sp', wmi[:], C.wbm[l], reads=[f'wbm{l}'], writes=['wmi'])
        P.dma('sp', wmo[:], C.wbw[l], reads=[f'wbw{l}'], writes=['wmo'])
        P.add('dve', lambda e: e.memset(ssb[:], 0.0), writes=[f'ss{i}' for i in range(40)])
        st_ = dict(nblk=0, ne=0, nres=0, invk=-1)

        def n_tile(k):
            return (C.yp, 0, 1) if k == 8 else (C.ys, k * 512, 0)

        def n_ld(k, b):
            src, row0, w = n_tile(k)
            slot = (k * 4 + b) % 2
            P.dma('sp', xin[:, slot, :], src[row0 + b * 128: row0 + (b + 1) * 128, :],
                  reads=[f'X{w}_{row0 + b * 128}'], writes=[f'xin{slot}'])

        def n_p1(k, b):
            src, row0, w = n_tile(k)
            gb = k * 4 + b
            slot = gb % 2
            _norm_p1(C, xin[:, slot, :], ssb[:, gb:gb + 1], rsb[:, gb:gb + 1], xn[:, slot, :], (f'xin{slot}', f'xn{slot}', 'hT'), gb)

        def n_p2(k, b):
            src, row0, w = n_tile(k)
            gb = k * 4 + b
            slot = gb % 2
            _norm_p2(C, xn[:, slot, :], lambda fc, b=b: hT[:, fc, b * 128:(b + 1) * 128], w, 1, (f'xin{slot}', f'xn{slot}', 'hT'), gb)

        def m0(k):
            n_ld(k, 0)
            for b in range(4):
                if b + 1 < 4:
                    n_ld(k, b + 1)
                n_p1(k, b)
                n_p2(k, b)

        def rope_ld(rc):
            P.dma('sp', rope[:, 0, :], C.c_rope[0][:, rc:rc + 512], writes=['rope'])
            P.dma('sp', rope[:, 1, :], C.c_rope[1][:, rc:rc + 512], writes=['rope'])

        def proj(co):
            pb = _psn(C)
            _mm_group(P, C.ps[pb][:, :], [(wmi[:, co // 128, kc, :], hT[:, kc, :]) for kc in range(8)],
                      reads=['wmi'] + [f'hT{k}' for k in range(8)], writes=[f'ps{pb}'])
            return pb

        def qk_evac(pq, gcol, RT, dst_fn, rope_col, dst_res, f32_dst=None):
            ps = C.ps[pq]
            C.qcall = getattr(C, 'qcall', -1) + 1
            rr = C.qcall % 2
            t1, t2, rsq, sqb, qb = t1r[:, rr], t2r[:, rr], rsqr[:, rr], sqbr[:, rr], qbr[:, rr]
            n_t1, n_t2, n_rsq, n_sqb, n_qb = f't1_{rr}', f't2_{rr}', f'rsq_{rr}', f'sqb_{rr}', f'qb_{rr}'
            P.add('act', lambda e: e.activation(sqb, ps[:, :], AF.Square), reads=[f'ps{pq}'], writes=[n_sqb])
            P.add('act', lambda e: e.copy(qb, ps[:, :]), reads=[f'ps{pq}'], writes=[n_qb])
            p1 = _psn(C)
            p2 = _psn(C)
            _mm_group(P, C.ps[p1][:, :], [(C.bonesb[:], sqb)], reads=['bonesb', n_sqb], writes=[f'ps{p1}'])
            _mm_group(P, C.ps[p2][:, :], [(RT[:], qb)], reads=['RqT', 'RkT', n_qb], writes=[f'ps{p2}'])
            P.add('dve', lambda e: e.scalar_tensor_tensor(t1, ps[:, :], C.gqk[:, gcol:gcol + 1], rope[:, 0, :], ALU.mult, ALU.mult),
                  reads=[f'ps{pq}', 'gqk', 'rope'], writes=[n_t1])
            P.add('dve', lambda e: e.tensor_tensor(t2, C.ps[p2][:, :], rope[:, 1, :], ALU.mult),
                  reads=[f'ps{p2}', 'rope'], writes=[n_t2])
            P.add('pool', lambda e: e.tensor_tensor(t1, t1, t2, ALU.add), reads=[n_t1, n_t2], writes=[n_t1])
            P.add('act', lambda e: e.activation(rsq, C.ps[p1][:, :], AF.Sqrt, bias=C.epsc[:, 0:1], scale=1.0 / 64),
                  reads=[f'ps{p1}', 'epsc'], writes=[n_rsq])
            P.add('dve', lambda e: e.reciprocal(rsq, rsq), reads=[n_rsq], writes=[n_rsq])
            if f32_dst is None:
                P.add('pool', lambda e: e.tensor_tensor(dst_fn(), t1, rsq, ALU.mult), reads=[n_t1, n_rsq], writes=[dst_res])
            else:
                P.add('dve', lambda e: e.tensor_tensor(f32_dst, t1, rsq, ALU.mult), reads=[n_t1, n_rsq], writes=['ucv'])
                P.add('pool', lambda e: e.tensor_copy(dst_fn(), f32_dst), reads=['ucv'], writes=[dst_res])

        def m1(tok0, slot, w, prompt, l=l):
            rc = T_S if prompt else tok0
            for c in range(4):
                pq = proj(c * 128)
                qk_evac(pq, 0, C.RqT, lambda c=c: qT[:, slot, c, :], rc, f'qT{slot}_{c}')
            pk = proj(512)
            if prompt:
                kf = ucv
                qk_evac(pk, 1, C.RkT, lambda: kT[:, tok0:tok0 + 512], rc, f'kT{tok0 // 512}', f32_dst=kf[:])
                P.add('act', lambda e: e.copy(khl[:, 0, :], kf[:]), reads=['ucv'], writes=['khi'])
                P.add('dve', lambda e: e.tensor_tensor(khl[:, 1, :], kf[:], khl[:, 0, :], ALU.subtract), reads=['ucv', 'khi'], writes=['klo'])
                pb = _psn(C)
                psb = C.ps[pb][:].bitcast(BF16)

                def trk(e):
                    ins = None
                    for hl in range(2):
                        for b in range(4):
                            ins = e.transpose(psb[:, hl * 512 + b * 128: hl * 512 + (b + 1) * 128], khl[:, hl, b * 128:(b + 1) * 128],
                                              C.identb[:])
                    return ins
                P.add('pe', trk, reads=['khi', 'klo', 'identb'], writes=[f'ps{pb}'])
                P.add('act', lambda e: e.copy(kout, psb[:, 0:512].rearrange("p (b f) -> p b f", b=4)),
                      reads=[f'ps{pb}'], writes=['acc0'])
                P.add('dve', lambda e: e.tensor_tensor(kout, kout, psb[:, 512:1024].rearrange("p (b f) -> p b f", b=4), ALU.add),
                      reads=[f'ps{pb}', 'acc0'], writes=['acc0'])
                for pi in range(2):
                    P.dma('sp', C.nk[pi, l].rearrange("(b p) f -> p b f", p=128), kout[:, pi * 2:(pi + 1) * 2, :],
                          reads=['acc0'], writes=[f'nk{pi}{l}'])
            else:
                qk_evac(pk, 1, C.RkT, lambda: kT[:, tok0:tok0 + 512], rc, f'kT{tok0 // 512}')
            pb = _psn(C)
            for b in range(4):
                _mm_group(P, C.ps[pb][:, b * 128:(b + 1) * 128],
                          [(hT[:, kc, b * 128:(b + 1) * 128], wmi[:, 5, kc, :]) for kc in range(8)],
                          reads=['wmi'] + [f'hT{k}' for k in range(8)], writes=[f'ps{pb}'])
            blk0 = tok0 // 128
            P.add('act', lambda e: e.copy(vS[:, blk0:blk0 + 4, :], C.ps[pb][:, :].rearrange("p (b f) -> p b f", b=4)),
                  reads=[f'ps{pb}'], writes=[f'vS{tok0 // 512}'])
            if prompt:
                P.add('dve', lambda e: e.tensor_copy(vout, C.ps[pb][:, :].rearrange("p (b f) -> p b f", b=4)),
                      reads=[f'ps{pb}'], writes=['acc1'])
                for pi in range(2):
                    P.dma('sp', C.nv[pi, l].rearrange("(b p) f -> p b f", p=128), vout[:, pi * 2:(pi + 1) * 2, :],
                          reads=['acc1'], writes=[f'nv{pi}{l}'])
            for c in range(2):
                pu = proj(768 + c * 128)
                if prompt:
                    for i in range(2):
                        P.add('act', lambda e, c=c, i=i, pu=pu: e.copy(uP[:, i, c, 8:264], C.ps[pu][:, i * 256:(i + 1) * 256]),
                              reads=[f'ps{pu}'], writes=[f'uP{i}'])
                else:
                    P.add('act', lambda e, c=c, pu=pu: e.copy(uP[:, slot, c, 8:520], C.ps[pu][:, :]),
                          reads=[f'ps{pu}'], writes=[f'uP{slot}'])
            for c in range(2):
                pu = proj(1024 + c * 128)
                P.add('act', lambda e, pu=pu: e.copy(ucv[:], C.ps[pu][:, :]), reads=[f'ps{pu}'], writes=['ucv'])
                pc = proj(1536 + c * 128)
                if prompt:
                    for i in range(2):
                        P.add('dve', lambda e, c=c, i=i, pc=pc: e.tensor_tensor(zc[:, i, c, 1:257], C.ps[pc][:, i * 256:(i + 1) * 256],
                                                                                 ucv[:, i * 256:(i + 1) * 256], ALU.mult),
                              reads=[f'ps{pc}', 'ucv'], writes=[f'zc{i}'])
                else:
                    P.add('dve', lambda e, c=c, pc=pc: e.tensor_tensor(zc[:, slot, c, 1:513], C.ps[pc][:, :], ucv[:], ALU.mult),
                          reads=[f'ps{pc}', 'ucv'], writes=[f'zc{slot}'])
            for c in range(2):
                pg = proj(1280 + c * 128)
                P.add('act', lambda e, c=c, pg=pg: e.copy(bgT[:, slot, c, :], C.ps[pg][:, :]), reads=[f'ps{pg}'], writes=[f'bgT{slot}'])

        def attention(slot, qcol_blocks, nxt=None, g_list=(0, 1)):
            pend = None
            if nxt is not None:
                n_ld(nxt, 0)
                n_ld(nxt, 1)
            for (b, chunks) in qcol_blocks:
                for g in g_list:
                    if nxt is not None and g == 0:
                        n_p1(nxt, b)
                        if b + 2 < 4:
                            n_ld(nxt, b + 2)
                    cur = att_a(slot, b, g, chunks)
                    if pend is not None:
                        att_b(*pend)
                    pend = cur
                    if nxt is not None and g == 1:
                        n_p2(nxt, b)
            att_b(*pend)

        def att_a(slot, b, g, chunks):
            gs = slice(g * 64, (g + 1) * 64)
            rhs_q = qT[gs, slot, :, b * 128:(b + 1) * 128]
            ech = []
            for (kfn, vap, mi, rds) in chunks:
                pb = _psn(C)
                out = C.ps[pb][:, :].rearrange("p (c t) -> p c t", c=4)

                def smm(e, out=out, kap=kfn(gs), rq=rhs_q, mi=mi):
                    ins = e.matmul(out, kap, rq, start=True, stop=(mi is None))
                    if mi is not None:
                        ins = e.matmul(out, C.identb[:], C.maskb[:, mi, :].rearrange("p (c t) -> p c t", c=4),
                                       start=False, stop=True)
                    return ins
                P.add('pe', smm, reads=rds + [f'qT{slot}_{c}' for c in range(4)] + ['identb', 'maskb'],
                      writes=[f'ps{pb}'])
                es = st_['ne'] % NET
                st_['ne'] += 1
                P.add('act', lambda e, es=es, pb=pb: e.activation(eT[:, es, :], C.ps[pb][:, :], AF.Exp, scale=0.125),
                      reads=[f'ps{pb}'], writes=[f'eT{es}'])
                ech.append((es, vap, rds))
            return (b, g, ech)

        def att_b(b, g, ech):
            gs = slice(g * 64, (g + 1) * 64)
            po = _psn(C)
            pd = _psn(C)
            _mm_group(P, C.ps[po][:, :], [(vap, eT[:, es, :]) for (es, vap, rds) in ech],
                      reads=[f'eT{es}' for (es, _, _) in ech] + sum([r for (_, _, r) in ech], []) + ['cvS'],
                      writes=[f'ps{po}'])
            _mm_group(P, C.ps[pd][:, :], [(C.onesb[:], eT[:, es, :]) for (es, vap, rds) in ech],
                      reads=[f'eT{es}' for (es, _, _) in ech] + ['onesb'], writes=[f'ps{pd}'])
            dg = st_['ne'] % 2
            P.add('dve', lambda e: e.tensor_tensor(dsum[gs, :], C.ps[pd][gs, :], C.sinkexp[gs, g * 512:(g + 1) * 512], ALU.add),
                  reads=[f'ps{pd}', 'sinkexp'], writes=[f'dsum{g}'])
            P.add('dve', lambda e: e.reciprocal(dsum[gs, :], dsum[gs, :]), reads=[f'dsum{g}'], writes=[f'dsum{g}'])
            P.add('dve', lambda e: e.tensor_tensor(
                mixT[gs, 0:4, b * 128:(b + 1) * 128], C.ps[po][gs, :].rearrange("p (c t) -> p c t", c=4),
                dsum[gs, :].rearrange("p (c t) -> p c t", c=4), ALU.mult),
                reads=[f'ps{po}', f'dsum{g}'], writes=[f'mixA{g}'])

        def poolconv2(n, mcol):
            for c in range(2):
                pb = _psn(C)
                _mm_group(P, C.ps[pb][:, 0:n], [(C.pwbd[:, c, :], plb[:, c, 0:n])], reads=['pwbd', 'plb'], writes=[f'ps{pb}'])
                P.add('dve', lambda e, c=c, pb=pb: e.tensor_scalar(mixT[:, 4 + c, mcol:mcol + n], C.ps[pb][:, 0:n],
                                                                   C.psc[:, c:c + 1], None, ALU.mult),
                      reads=[f'ps{pb}', 'psc'], writes=[f'mixP{c}'])

        def poolconv(uslot, n, mcol, kind, bg_ap_fn, stage=0):
            if stage == 2:
                return poolconv2(n, mcol)
            if st_['invk'] != kind:
                P.dma('sp', invc[:], C.c_invc[kind], writes=['invc'])
                st_['invk'] = kind
            icol = mcol if kind == 3 else 0
            U = uP[:, uslot]
            ur = f'uP{uslot}'
            P.add('pool', lambda e: e.tensor_tensor(pA[:, :, 1:n + 16], U[:, :, 0:n + 15], U[:, :, 1:n + 16], ALU.add), reads=[ur], writes=['pA'])
            P.add('pool', lambda e: e.tensor_tensor(pB[:, :, 2:n + 15], pA[:, :, 1:n + 14], pA[:, :, 3:n + 16], ALU.add), reads=['pA'], writes=['pB'])
            P.add('pool', lambda e: e.tensor_tensor(pA[:, 1, 4:n + 13], pB[:, 1, 2:n + 11], pB[:, 1, 6:n + 15], ALU.add), reads=['pB', 'pA'], writes=['pA'])
            P.add('pool', lambda e: e.tensor_tensor(pB[64:128, 1, 8:n + 8], pA[64:128, 1, 4:n + 4], pA[64:128, 1, 12:n + 12], ALU.add),
                  reads=['pA', 'pB'], writes=['pB'])
            P.add('pool', lambda e: e.tensor_copy(pA[64:128, :, 8:8 + n], pB[64:128, :, 8:8 + n]), reads=['pB', 'pA'], writes=['pA'])
            P.add('pool', lambda e: e.tensor_tensor(pA[:, :, 8:8 + n], pA[:, :, 8:8 + n], invc[:, :, icol:icol + n], ALU.mult),
                  reads=['pA', 'invc'], writes=['pA'])
            P.add('pool', lambda e: e.tensor_tensor(plb[:, :, 0:n], pA[:, :, 8:8 + n], U[:, :, 8:8 + n], ALU.subtract),
                  reads=['pA', ur], writes=['plb'])
            if stage == 0:
                poolconv2(n, mcol)
            Z = zc[:, uslot]
            zr = f'zc{uslot}'
            for c in range(2):
                P.add('dve', lambda e, c=c: e.tensor_scalar(acc[:, c, 0:n], Z[:, c, 0:n], C.cw[:, c, 0:1], None, ALU.mult),
                      reads=[zr, 'cw'], writes=[f'acc{c}'])
                P.add('dve', lambda e, c=c: e.scalar_tensor_tensor(acc[:, c, 0:n], Z[:, c, 1:n + 1], C.cw[:, c, 1:2], acc[:, c, 0:n],
                                                                     ALU.mult, ALU.add), reads=[zr, 'cw', f'acc{c}'], writes=[f'acc{c}'])
                P.add('dve', lambda e, c=c: e.scalar_tensor_tensor(acc[:, c, 0:n], Z[:, c, 2:n + 2], C.cw[:, c, 2:3], acc[:, c, 0:n],
                                                                     ALU.mult, ALU.add), reads=[zr, 'cw', f'acc{c}'], writes=[f'acc{c}'])
                P.add('pool', lambda e, c=c: e.tensor_tensor(mixT[:, 6 + c, mcol:mcol + n], acc[:, c, 0:n], bg_ap_fn(c), ALU.mult),
                      reads=[f'acc{c}', 'bgT0', 'bgT1'], writes=[f'mixC{c}'])

        def wout_stage(dst_src, row0, w):
            def ld(b):
                slot = (st_['nres'] + b) % 2
                P.dma('sp', xres[:, slot, :], dst_src[0][row0 + b * 128: row0 + (b + 1) * 128, :],
                      reads=[f'X{w}_{row0 + b * 128}'], writes=[f'xres{slot}'])
            ld(0)
            mix_reads = ['mixA0', 'mixA1', 'mixP0', 'mixP1', 'mixC0', 'mixC1', 'wmo']
            for b in range(4):
                if b + 1 < 4:
                    ld(b + 1)
                slot = (st_['nres'] + b) % 2
                for hf in range(2):
                    pb = _psn(C)
                    fs = slice(hf * 512, (hf + 1) * 512)
                    _mm_group(P, C.ps[pb][:, :], [(mixT[:, c, b * 128:(b + 1) * 128], wmo[:, c, fs]) for c in range(8)],
                              reads=mix_reads, writes=[f'ps{pb}'])
                    P.add('dve', lambda e, pb=pb, fs=fs, hf=hf: e.tensor_tensor(tmp[:, hf, :], C.ps[pb][:, :], C.gate[:, fs], ALU.mult),
                          reads=[f'ps{pb}', 'gate'], writes=[f'tmpm{hf}'])
                    P.add('pool', lambda e, fs=fs, slot=slot, hf=hf: e.tensor_tensor(xres[:, slot, fs], tmp[:, hf, :], xres[:, slot, fs], ALU.add),
                          reads=[f'tmpm{hf}', f'xres{slot}'], writes=[f'xres{slot}'])
                P.dma('sp', dst_src[1][row0 + b * 128: row0 + (b + 1) * 128, :], xres[:, slot, :], reads=[f'xres{slot}'],
                      writes=[f'X{w}_{row0 + b * 128}'])
            st_['nres'] += 4

        def m2_sample(t, nxt=None):
            slot = t % 2
            blocks = []
            for b in range(4):
                gb = 4 * t + b
                chunks = []
                for kc in range(4):
                    chunks.append((lambda gs, kc=kc: C.ckT[gs, kc * 128:(kc + 1) * 128], C.cvS[:, kc, :], None, ['ckT']))
                for (blk, mi) in ((gb - 1, 0), (gb, None), (gb + 1, 1)):
                    if 0 <= blk < 32:
                        chunks.append((lambda gs, blk=blk: kT[gs, blk * 128:(blk + 1) * 128], vS[:, blk, :], mi,
                                       [f'kT{blk // 4}', f'vS{blk // 4}']))
                blocks.append((b, chunks))
            kind = 0 if t == 0 else (2 if t == 7 else 1)
            poolconv(slot, 512, 0, kind, lambda c: bgT[:, slot, c, :], stage=1)
            attention(slot, blocks, nxt)
            poolconv(slot, 512, 0, kind, lambda c: bgT[:, slot, c, :], stage=2)
            wout_stage((C.ys, C.ys), t * 512, 0)

        try:
            _mixer_main(C, m0, m1, m2_sample, attention, poolconv, wout_stage, uP, zc, kT, vS, bgT, rope_ld)
        except _Stop:
            pass


def _mixer_main(C, m0, m1, m2_sample, attention, poolconv, wout_stage, uP, zc, kT, vS, bgT, rope_ld):
    P = C.P
    if True:
        rope_ld(0)
        m0(0)
        for t in range(8):
            s = t % 2
            m1(t * 512, s, 0, False)
            rope_ld((t + 1) * 512 if t < 7 else T_S)
            if t == 0:
                P.add('pool', lambda e: e.memset(uP[:, 0, :, 0:8], 0.0), reads=['uP0'], writes=['uP0'])
                P.add('pool', lambda e: e.memset(zc[:, 0, :, 0:1], 0.0), reads=['zc0'], writes=['zc0'])
            else:
                o = 1 - s
                P.add('pool', lambda e, s=s, o=o: e.tensor_copy(uP[:, s, :, 0:8], uP[:, o, :, 512:520]), reads=[f'uP{o}', f'uP{s}'], writes=[f'uP{s}'])
                P.add('pool', lambda e, s=s, o=o: e.tensor_copy(uP[:, o, :, 520:528], uP[:, s, :, 8:16]), reads=[f'uP{o}', f'uP{s}'], writes=[f'uP{o}'])
                P.add('pool', lambda e, s=s, o=o: e.tensor_copy(zc[:, s, :, 0:1], zc[:, o, :, 512:513]), reads=[f'zc{o}', f'zc{s}'], writes=[f'zc{s}'])
                P.add('pool', lambda e, s=s, o=o: e.tensor_copy(zc[:, o, :, 513:514], zc[:, s, :, 1:2]), reads=[f'zc{o}', f'zc{s}'], writes=[f'zc{o}'])
            if t == 7:
                P.add('pool', lambda e, s=s: e.memset(uP[:, s, :, 520:528], 0.0), reads=[f'uP{s}'], writes=[f'uP{s}'])
                P.add('pool', lambda e, s=s: e.memset(zc[:, s, :, 513:514], 0.0), reads=[f'zc{s}'], writes=[f'zc{s}'])
            if t >= 1:
                m2_sample(t - 1, t + 1)
            else:
                m0(1)
        m2_sample(7, None)
        _load_gate(C, C.cur_l, 1, 1)
        for i in range(2):
            P.add('pool', lambda e, i=i: e.memset(uP[:, i, :, :], 0.0), reads=[f'uP{i}'], writes=[f'uP{i}'])
            P.add('pool', lambda e, i=i: e.memset(zc[:, i, :, :], 0.0), reads=[f'zc{i}'], writes=[f'zc{i}'])
        m1(T_S, 0, 1, True)
        blocks = []
        for b in range(4):
            sq = b // 2
            chunks = []
            for kb in range(2):
                blk = 32 + sq * 2 + kb
                chunks.append((lambda gs, blk=blk: kT[gs, blk * 128:(blk + 1) * 128], vS[:, blk, :], None, ['kT8', 'vS8']))
            blocks.append((b, chunks))
        attention(0, blocks)
        for i in range(2):
            poolconv(i, 256, i * 256, 3, lambda c, i=i: bgT[:, 0, c, i * 256:(i + 1) * 256])
        wout_stage((C.yp, C.yp), 0, 1)


def kernel(**inputs):
    f = lambda k: np.ascontiguousarray(np.asarray(inputs[k], dtype=np.float32))
    x_prompt, x_sample = f('x_prompt'), f('x_sample')
    cache_k, cache_v, c, c_ctx = f('cache_k'), f('cache_v'), f('c'), f('c_ctx')
    shared = {k: f(k) for k in ('w_ada', 'b_ada', 'norm_g', 'w_ffn_in', 'w_ffn_out', 'w_in', 'w_out', 'q_norm_g',
                                'k_norm_g', 'sink', 'pool_w', 'pool_scale', 'conv_w')}
    shared.update(_host_constants())
    nc = build_program()
    in_maps = []
    for i in range(8):
        m = dict(shared)
        m['xs'] = x_sample[i]
        m['xp'] = np.ascontiguousarray(x_prompt[2 * i:2 * i + 2].reshape(T_P, D))
        m['ck'] = np.ascontiguousarray(cache_k[i].reshape(NL, 512, 128))
        m['cv'] = np.ascontiguousarray(cache_v[i].reshape(NL, 512, 128))
        m['cond'] = np.ascontiguousarray(np.stack([c[i], c_ctx]))
        in_maps.append(m)
    res = run_bass_kernel_spmd(nc, in_maps, core_ids=list(range(8)))
    r = res.results
    ys = np.stack([np.asarray(r[i]['ys'], np.float32) for i in range(8)])
    yp = np.concatenate([np.asarray(r[i]['yp'], np.float32).reshape(2, 256, D) for i in range(8)], axis=0)
    nk = np.concatenate([np.asarray(r[i]['nk'], np.float32) for i in range(8)], axis=0).reshape(16, NL, 256, 2, 64)
    nv = np.concatenate([np.asarray(r[i]['nv'], np.float32) for i in range(8)], axis=0).reshape(16, NL, 256, 2, 64)
    return (yp, ys, nk, nv)
```

```python
import numpy as np
import ml_dtypes
import concourse.bass as bass
import concourse.mybir as mybir
from concourse.bass_utils import run_bass_kernel_spmd

F32 = mybir.dt.float32
BF16 = mybir.dt.bfloat16
AF = mybir.ActivationFunctionType
ALU = mybir.AluOpType

ENGS = ('pe', 'act', 'dve', 'pool', 'sp')
DMA_RING = {'sp': 24, 'pool': 12, 'act': 8}
STRICT_SAME_ENGINE = True


class _Op:
    __slots__ = ('eng', 'fn', 'deps', 'need_sig', 'sig', 'is_dma', 'dsem', 'dval', 'idx', 'bg')

    def __init__(self, eng, fn, is_dma):
        self.eng = eng
        self.fn = fn
        self.deps = []
        self.need_sig = False
        self.sig = 0
        self.is_dma = is_dma
        self.dsem = None
        self.dval = 0
        self.idx = 0
        self.bg = False


class Prog:
    def __init__(self, nc):
        self.nc = nc
        self.ops = {e: [] for e in ENGS}
        self.last_w = {}
        self.readers = {}
        self.dma_hist = {q: [] for q in DMA_RING}
        self.keep = set()
        self.n = 0

    def _track(self, op, reads, writes, extra):
        deps = {}
        for r in reads:
            w = self.last_w.get(r)
            if w is not None:
                deps[id(w)] = (w, True)
            if r.startswith('ps'):
                for rd in self.readers.get(r, ()):
                    if rd.eng != op.eng:
                        deps[id(rd)] = (rd, True)
        for w_ in writes:
            lw = self.last_w.get(w_)
            if lw is not None and id(lw) not in deps:
                deps[id(lw)] = (lw, False)
            for rd in self.readers.get(w_, ()):
                if id(rd) not in deps:
                    deps[id(rd)] = (rd, False)
        for e_ in extra:
            deps[id(e_)] = (e_, True)
        for d, raw in deps.values():
            if d is op:
                continue
            if d.eng == op.eng and not d.is_dma and not op.is_dma:
                if op.eng == 'pe' or (not raw and not STRICT_SAME_ENGINE):
                    continue
            op.deps.append(d)
            d.need_sig = True
        for w_ in writes:
            self.last_w[w_] = op
            self.readers[w_] = []
        for r in reads:
            if r not in writes:
                self.readers.setdefault(r, []).append(op)

    def add(self, eng, fn, reads=(), writes=(), extra=()):
        op = _Op(eng, fn, False)
        op.idx = self.n
        self.n += 1
        self._track(op, reads, writes, extra)
        self.ops[eng].append(op)
        return op

    def dma(self, q, out, in_, reads=(), writes=(), extra=(), nonc=False, bg=False):
        if nonc:
            op = _Op(q, lambda e: e.dma_start(out=out, in_=in_, allow_slow_non_contiguous=True), True)
        else:
            op = _Op(q, lambda e: e.dma_start(out=out, in_=in_), True)
        op.idx = self.n
        op.bg = bg
        if bg:
            self.keep.update(writes)
        self.n += 1
        hist = self.dma_hist[q]
        k = DMA_RING[q]
        ex = list(extra)
        if len(hist) >= k:
            ex.append(hist[len(hist) - k])
        self._track(op, reads, writes, ex)
        op.dsem = len(hist) % k
        op.dval = 16 * (len(hist) // k + 1)
        hist.append(op)
        self.ops[q].append(op)
        return op

    def barrier(self):
        lasts = []
        for e in ENGS:
            for op in reversed(self.ops[e]):
                if not op.bg:
                    lasts.append(op)
                    break
        dmas = []
        for q in DMA_RING:
            dmas += [d for d in self.dma_hist[q][-DMA_RING[q]:] if not d.bg]
        for e in ENGS:
            self.add(e, None, extra=[l for l in lasts if l.eng != e or l.is_dma] + dmas)
        self.last_w = {k: v for k, v in self.last_w.items() if k in self.keep}
        self.readers = {}

    def finish(self):
        nc = self.nc
        dmas = []
        for q in DMA_RING:
            dmas += self.dma_hist[q][-DMA_RING[q]:]
        self.add('sp', None, extra=dmas)
        for e in ENGS:
            c = 0
            for op in self.ops[e]:
                if op.need_sig and not op.is_dma:
                    c += 1
                    op.sig = c
        import contextlib
        with contextlib.ExitStack() as st:
            esem = {e: st.enter_context(nc.semaphore("s_" + e)) for e in ENGS}
            dsem = {q: [st.enter_context(nc.semaphore(f"d_{q}{i}")) for i in range(DMA_RING[q])] for q in DMA_RING}
            block = st.enter_context(nc.Block())
            ops = self.ops

            def run(e, eng):
                seen = {}
                for op in ops[e]:
                    waits = {}
                    for d in op.deps:
                        if d.is_dma:
                            key = ('d', d.eng, d.dsem)
                            val = d.dval
                        else:
                            key = ('e', d.eng)
                            val = d.sig
                        if waits.get(key, 0) < val:
                            waits[key] = val
                    for key, val in waits.items():
                        if seen.get(key, 0) >= val:
                            continue
                        seen[key] = val
                        sem = esem[key[1]] if key[0] == 'e' else dsem[key[1]][key[2]]
                        eng.wait_ge(sem, val)
                    if op.fn is None:
                        if op.need_sig:
                            eng.nop().then_inc(esem[e], 1)
                        continue
                    ins = op.fn(eng)
                    if op.is_dma:
                        ins.then_inc(dsem[e][op.dsem], 16)
                    elif op.need_sig:
                        ins.then_inc(esem[e], 1)

            @block.tensor
            def _(eng):
                run('pe', eng)

            @block.scalar
            def _(eng):
                run('act', eng)

            @block.vector
            def _(eng):
                run('dve', eng)

            @block.gpsimd
            def _(eng):
                run('pool', eng)

            @block.sync
            def _(eng):
                run('sp', eng)


D = 1024
DFF = 2816
NJ = 22
T_S = 4096
T_P = 512
NL = 2
EPS = 1e-6
NEGBIG = -30000.0
IN_W = 1792
SQ = 512
NET = 14
GRID_W = 64


class _C:
    pass


_SBN = [0]


def _sbt(nc, name, shape, dt, **kw):
    _SBN[0] += 1
    return nc.sbuf_tensor(f"{name}_{_SBN[0]}", shape, dt, align_bytes=256)


class _Stop(Exception):
    pass


def _mm_group(P, out, pairs, reads, writes, extra=()):
    n = len(pairs)

    def fn(e):
        ins = None
        for i, (l, r) in enumerate(pairs):
            ins = e.matmul(out, l, r, start=(i == 0), stop=(i == n - 1))
        return ins
    return P.add('pe', fn, reads=reads, writes=writes, extra=extra)


def _host_constants():
    ident = np.eye(128, dtype=np.float32)
    jj = np.arange(128)[:, None]
    ii = np.arange(128)[None, :]
    m_prev = np.where(jj >= ii, 0.0, NEGBIG).astype(np.float32)
    m_next = np.where(jj <= ii, 0.0, NEGBIG).astype(np.float32)
    mask = np.stack([np.tile(m_prev, (1, 4)), np.tile(m_next, (1, 4))]).astype(np.float32)
    half = 32
    inv = (10000.0 ** (-np.arange(0, half, 2, dtype=np.float32) / half)).astype(np.float32)
    t = np.arange(T_S)
    row = (t // GRID_W).astype(np.float32)
    col = (t % GRID_W).astype(np.float32)
    cos = np.ones((128, T_S + T_P), np.float32)
    sin = np.zeros((128, T_S + T_P), np.float32)
    for p in range(128):
        d = p % 64
        pos = row if d < 32 else col
        ang = (pos * inv[d % 16]).astype(np.float32)
        cos[p, :T_S] = np.cos(ang)
        sin[p, :T_S] = np.sin(ang)
    rope = np.stack([cos, sin]).astype(np.float32)
    R = np.zeros((128, 128), np.float32)
    for p in range(128):
        if (p % 32) < 16:
            R[p, p + 16] = -1.0
        else:
            R[p, p - 16] = 1.0
    rt = np.ascontiguousarray(R.T)
    bones = np.zeros((128, 128), np.float32)
    bones[:64, :64] = 1.0
    bones[64:, 64:] = 1.0
    wins = (2, 4, 8, 16)
    invc = np.zeros((4, 128, 2, 512), np.float32)

    def cnt(tt, T, w):
        lo = np.clip(tt - w // 2, 0, T)
        hi = np.clip(tt + w // 2, 0, T)
        return (hi - lo).astype(np.float32)
    for kind in range(4):
        for c in range(2):
            for h in range(2):
                w = wins[2 * c + h]
                if kind == 3:
                    tt = np.arange(512) % 256
                    cn = cnt(tt, 256, w)
                else:
                    base = {0: 0, 1: 1024, 2: T_S - 512}[kind]
                    cn = cnt(base + np.arange(512), T_S, w)
                invc[kind, h * 64:(h + 1) * 64, c, :] = (1.0 / cn)[None, :]
    return dict(c_ident=ident, c_mask=mask, c_rope=rope, c_rt=rt, c_bones=bones, c_invc=invc)


def build_program(stop_after=None):
    nc = bass.Bass("TRN2", target_bir_lowering=False)
    C = _C()
    C.nc = nc
    P = Prog(nc)
    C.P = P
    C.pb = 0
    C.peng = 'dve'

    def din(name, shape):
        return nc.dram_tensor(name, list(shape), F32, kind="ExternalInput").ap()

    def dout(name, shape):
        return nc.dram_tensor(name, list(shape), F32, kind="ExternalOutput").ap()

    C.xs = din("xs", [T_S, D])
    C.xp = din("xp", [T_P, D])
    C.ck = din("ck", [NL, 512, 128])
    C.cv = din("cv", [NL, 512, 128])
    C.cond = din("cond", [2, D])
    C.w_ada = din("w_ada", [NL, D, 9 * D])
    C.b_ada = din("b_ada", [NL, 9 * D])
    C.norm_g = din("norm_g", [NL, 3, D])
    C.w_ffn_in = din("w_ffn_in", [NL, 2, D, 2 * DFF])
    C.w_ffn_out = din("w_ffn_out", [NL, 2, DFF, D])
    C.w_in = din("w_in", [NL, D, IN_W])
    C.w_out = din("w_out", [NL, D, D])
    C.q_norm_g = din("q_norm_g", [NL, 64])
    C.k_norm_g = din("k_norm_g", [NL, 64])
    C.sink = din("sink", [NL, 8])
    C.pool_w = din("pool_w", [NL, 4, 64, 64])
    C.pool_scale = din("pool_scale", [NL, 256])
    C.conv_w = din("conv_w", [NL, 3, 256])
    C.c_ident = din("c_ident", [128, 128])
    C.c_mask = din("c_mask", [2, 128, 512])
    C.c_rope = din("c_rope", [2, 128, T_S + T_P])
    C.c_rt = din("c_rt", [128, 128])
    C.c_bones = din("c_bones", [128, 128])
    C.c_invc = din("c_invc", [4, 128, 2, 512])
    C.ys = dout("ys", [T_S, D])
    C.yp = dout("yp", [T_P, D])
    C.nk = dout("nk", [2, NL, 256, 128])
    C.nv = dout("nv", [2, NL, 256, 128])
    C.wbi = nc.dram_tensor("wbi", [NL, 2, NJ, 128, 8, 256], BF16, kind="Internal").ap()
    C.wbo = nc.dram_tensor("wbo", [NL, 2, 128, NJ, D], BF16, kind="Internal").ap()
    C.wbm = nc.dram_tensor("wbm", [NL, 128, 14, 8, 128], BF16, kind="Internal").ap()
    C.wbw = nc.dram_tensor("wbw", [NL, 128, 8, D], BF16, kind="Internal").ap()
    C.modD = nc.dram_tensor("modD", [NL, 2, 9 * D], F32, kind="Internal").ap()

    import contextlib
    with contextlib.ExitStack() as st:
        def sb(name, shape, dt=F32):
            return st.enter_context(_sbt(nc, name, list(shape), dt))
        C.ps = [st.enter_context(nc.psum_tensor(f"ps{i}", [128, 512], F32)) for i in range(8)]
        C.identb = sb("identb", [128, 128], BF16)
        C.identf = sb("identf", [128, 128])
        C.maskb = sb("maskb", [128, 2, 512], BF16)
        C.bonesb = sb("bonesb", [128, 128], BF16)
        C.onesb = sb("onesb", [128, 128], BF16)
        C.rtf = sb("rtf", [128, 128])
        C.modA = sb("modA", [128, 2, 3, 8])
        C.modB = sb("modB", [128, 2, 3, 8])
        C.gqk = sb("gqk", [128, 2])
        C.RqT = sb("RqT", [128, 128], BF16)
        C.RkT = sb("RkT", [128, 128], BF16)
        C.sinkexp = sb("sinkexp", [128, 1024])
        C.cw = sb("cw", [128, 2, 3])
        C.psc = sb("psc", [128, 2])
        C.pwbd = sb("pwbd", [128, 2, 128], BF16)
        C.ckT = sb("ckT", [128, 512], BF16)
        C.cvS = sb("cvS", [128, 4, 128], BF16)
        C.gate = sb("gate", [128, D])
        C.epsc = sb("epsc", [128, 1])

        _emit_all(C, stop_after)
        P.finish()
    return nc


def _psn(C):
    i = C.pb
    C.pb = (C.pb + 1) % 8
    return i


def _emit_all(C, stop_after):
    nc, P = C.nc, C.P
    _consts(C)
    _conv_ffn(C, 0, 0)
    first = True
    for l in range(NL):
        _ada(C, [l])
        P.barrier()
        if l == 0:
            _conv_mix(C, 0)
            _conv_ffn(C, 0, 1)
            _conv_ffn(C, 1, 0)
            _conv_mix(C, 1)
            _conv_ffn(C, 1, 1)
        if stop_after == 'ada':
            return
        _layer_setup(C, l)
        P.barrier()
        if stop_after == 'lsetup':
            return
        _ffn_phase(C, l, 0, 0, first)
        first = False
        P.barrier()
        if stop_after == ('ffn', l, 0):
            return
        _mixer_phase(C, l)
        P.barrier()
        if stop_after == ('mix', l):
            return
        _ffn_phase(C, l, 1, 2, False)
        P.barrier()
        if stop_after == ('ffn', l, 1):
            return


def _conv_ffn(C, l, f):
    P = C.P
    for j in range(NJ):
        for gu in range(2):
            P.dma('pool', C.wbi[l, f, j][:, :, gu * 128:(gu + 1) * 128],
                  C.w_ffn_in[l, f][:, gu * DFF + j * 128: gu * DFF + (j + 1) * 128].rearrange(
                      "(kc p) c -> p kc c", p=128),
                  writes=[f'wbi{l}{f}_{j}'], bg=True)
    for h in range(2):
        P.dma('pool', C.wbo[l, f][:, h * 11:(h + 1) * 11, :],
              C.w_ffn_out[l, f][h * 11 * 128:(h + 1) * 11 * 128, :].rearrange("(jc p) n -> p jc n", p=128),
              writes=[f'wbo{l}{f}_{h}'], bg=True)


def _conv_mix(C, l):
    P = C.P
    for c in range(4):
        for hf in range(2):
            head = c + 4 * hf
            P.dma('pool', C.wbm[l][:, c, :, hf * 64:(hf + 1) * 64],
                  C.w_in[l][:, head * 64:(head + 1) * 64].rearrange("(kc p) c -> p kc c", p=128),
                  writes=[f'wbm{l}'], bg=True)
    for ch in range(4, 14):
        P.dma('pool', C.wbm[l][:, ch, :, :],
              C.w_in[l][:, ch * 128:(ch + 1) * 128].rearrange("(kc p) c -> p kc c", p=128), writes=[f'wbm{l}'], bg=True)
    for hf in range(2):
        P.dma('pool', C.wbw[l][hf * 64:(hf + 1) * 64, 0:4, :],
              C.w_out[l][hf * 256:(hf + 1) * 256, :].rearrange("(c p) n -> p c n", p=64), writes=[f'wbw{l}'], bg=True)
    P.dma('pool', C.wbw[l][:, 4:8, :], C.w_out[l][512:1024, :].rearrange("(c p) n -> p c n", p=128),
          writes=[f'wbw{l}'], bg=True)


def _consts(C):
    nc, P = C.nc, C.P
    P.dma('pool', C.identb[:], C.c_ident, writes=['identb'], bg=True)
    P.dma('pool', C.bonesb[:], C.c_bones, writes=['bonesb'], bg=True)
    for m in range(2):
        P.dma('pool', C.maskb[:, m, :], C.c_mask[m], writes=['maskb'], bg=True)
    P.dma('sp', C.identf[:], C.c_ident, writes=['identf'])
    P.dma('sp', C.rtf[:], C.c_rt, writes=['rtf'])
    P.add('dve', lambda e: e.memset(C.onesb[:], 1.0), writes=['onesb'])
    P.add('dve', lambda e: e.memset(C.epsc[:], EPS), writes=['epsc'])


def _ada(C, layers):
    nc, P = C.nc, C.P
    NS = 4
    with (_sbt(nc, "condT", [128, 8, 2], F32) as condT,
          _sbt(nc, "scT", [128, 8, 2], BF16) as scT,
          _sbt(nc, "wa", [128, NS, 8, 512], F32) as wa,
          _sbt(nc, "wab", [128, NS, 8, 512], BF16) as wab,
          _sbt(nc, "brow", [2, NS, 512], F32) as brow,
          _sbt(nc, "mrow", [2, NS, 512], F32) as mrow):
        for w in range(2):
            P.dma('sp', condT[:, :, w], C.cond[w].rearrange("(kc p) -> p kc", p=128), writes=[f'condT{w}'], nonc=True)
        P.add('act', lambda e: e.activation(scT[:], condT[:], AF.Silu), reads=['condT0', 'condT1'], writes=['scT'])
        items = [(l, cg) for l in layers for cg in range(18)]

        def issue(i):
            l, cg = items[i]
            s = i % NS
            P.dma('sp', wa[:, s], C.w_ada[l][:, cg * 512:(cg + 1) * 512].rearrange("(kc p) n -> p kc n", p=128),
                  writes=[f'wa{s}'])
            for w in range(2):
                P.dma('sp', brow[w:w + 1, s, :], C.b_ada[l:l + 1, cg * 512:(cg + 1) * 512], writes=[f'brow{s}_{w}'])

        for i in range(min(NS - 1, len(items))):
            issue(i)
        for i, (l, cg) in enumerate(items):
            if i + NS - 1 < len(items):
                issue(i + NS - 1)
            s = i % NS
            P.add('dve', lambda e, s=s: e.tensor_copy(wab[:, s, 0:4], wa[:, s, 0:4]), reads=[f'wa{s}'], writes=[f'wabA{s}'])
            P.add('act', lambda e, s=s: e.copy(wab[:, s, 4:8], wa[:, s, 4:8]), reads=[f'wa{s}'], writes=[f'wabB{s}'])
            pb = _psn(C)
            _mm_group(P, C.ps[pb][0:2, :], [(scT[:, kc, :], wab[:, s, kc, :]) for kc in range(8)],
                      reads=['scT', f'wabA{s}', f'wabB{s}'], writes=[f'ps{pb}'])
            P.add('dve', lambda e, s=s, pb=pb: e.tensor_tensor(mrow[:, s, :], C.ps[pb][0:2, :], brow[:, s, :], ALU.add),
                  reads=[f'ps{pb}', f'brow{s}_0', f'brow{s}_1'], writes=[f'mrow{s}'])
            P.dma('sp', C.modD[l][:, cg * 512:(cg + 1) * 512], mrow[:, s, :], reads=[f'mrow{s}'], writes=[f'modD{l}_{cg}'])


def _layer_setup(C, l):
    nc, P = C.nc, C.P
    with (_sbt(nc, "sc", [128, 2, 3, 8], F32) as sc,
          _sbt(nc, "g3", [128, 3, 8], F32) as g3,
          _sbt(nc, "sk", [128, 8], F32) as sk,
          _sbt(nc, "ske", [128, 8], F32) as ske,
          _sbt(nc, "pwf", [128, 2, 128], F32) as pwf,
          _sbt(nc, "ckf", [128, 4, 128], F32) as ckf,
          _sbt(nc, "ckb", [128, 4, 128], BF16) as ckb,
          _sbt(nc, "cvf", [128, 4, 128], F32) as cvf):
        for w in range(2):
            for s in range(3):
                P.dma('sp', sc[:, w, s, :], C.modD[l, w, (3 * s + 1) * D:(3 * s + 2) * D].rearrange("(c p) -> p c", p=128),
                      reads=[f'modD{l}'], writes=[f'sc{w}{s}'], nonc=True)
                P.dma('sp', C.modB[:, w, s, :], C.modD[l, w, (3 * s) * D:(3 * s + 1) * D].rearrange("(c p) -> p c", p=128),
                      reads=[f'modD{l}'], writes=[f'modB{w}{s}'], nonc=True)
        for s in range(3):
            P.dma('sp', g3[:, s, :], C.norm_g[l, s].rearrange("(c p) -> p c", p=128), writes=[f'g3{s}'], nonc=True)
        for w in range(2):
            for s in range(3):
                P.add('dve', lambda e, w=w, s=s: e.scalar_tensor_tensor(C.modA[:, w, s, :], sc[:, w, s, :], 1.0, g3[:, s, :],
                                                                        ALU.add, ALU.mult),
                      reads=[f'sc{w}{s}', f'g3{s}'], writes=[f'modA{w}{s}'])
        for hf in range(2):
            P.dma('sp', C.gqk[hf * 64:(hf + 1) * 64, 0:1], C.q_norm_g[l].rearrange("(p o) -> p o", o=1), writes=[f'gqk{hf}0'], nonc=True)
            P.dma('sp', C.gqk[hf * 64:(hf + 1) * 64, 1:2], C.k_norm_g[l].rearrange("(p o) -> p o", o=1), writes=[f'gqk{hf}1'], nonc=True)
        P.add('dve', lambda e: e.tensor_scalar(C.RqT[:], C.rtf[:], C.gqk[:, 0:1], None, ALU.mult), reads=['rtf', 'gqk00', 'gqk10'], writes=['RqT'])
        P.add('dve', lambda e: e.tensor_scalar(C.RkT[:], C.rtf[:], C.gqk[:, 1:2], None, ALU.mult), reads=['rtf', 'gqk01', 'gqk11'], writes=['RkT'])
        P.dma('sp', sk[:], C.sink[l].partition_broadcast(128), writes=['sk'], nonc=True)
        P.add('act', lambda e: e.activation(ske[:], sk[:], AF.Exp), reads=['sk'], writes=['ske'])
        P.add('dve', lambda e: e.memset(C.sinkexp[:], 0.0), writes=['sinkexp'])
        for h in range(8):
            P.add('dve', lambda e, h=h: e.tensor_scalar(C.sinkexp[:, h * 128:(h + 1) * 128], C.sinkexp[:, h * 128:(h + 1) * 128],
                                                        ske[:, h:h + 1], None, ALU.add),
                  reads=['ske', 'sinkexp'], writes=['sinkexp'])
        for k in range(3):
            P.dma('sp', C.cw[:, :, k], C.conv_w[l, k].rearrange("(c p) -> p c", p=128), writes=[f'cw{k}'], nonc=True)
        P.dma('sp', C.psc[:], C.pool_scale[l].rearrange("(c p) -> p c", p=128), writes=['psc'], nonc=True)
        P.add('dve', lambda e: e.memset(pwf[:], 0.0), writes=['pwf'])
        for c in range(2):
            for h in range(2):
                P.dma('sp', pwf[h * 64:(h + 1) * 64, c, h * 64:(h + 1) * 64], C.pool_w[l, 2 * c + h], reads=['pwf'], writes=[f'pwf{c}{h}'])
        P.add('dve', lambda e: e.tensor_copy(C.pwbd[:], pwf[:]), reads=['pwf', 'pwf00', 'pwf01', 'pwf10', 'pwf11'], writes=['pwbd'])
        P.dma('sp', ckf[:], C.ck[l].rearrange("(kc p) f -> p kc f", p=128), writes=['ckf'])
        P.dma('sp', cvf[:], C.cv[l].rearrange("(kc p) f -> p kc f", p=128), writes=['cvf'])
        P.add('dve', lambda e: e.tensor_copy(ckb[:], ckf[:]), reads=['ckf'], writes=['ckb'])
        P.add('dve', lambda e: e.tensor_copy(C.cvS[:], cvf[:]), reads=['cvf'], writes=['cvS'])
        pb = _psn(C)
        psb = C.ps[pb][:].bitcast(BF16)

        def tr(e):
            ins = None
            for kc in range(4):
                ins = e.transpose(psb[:, kc * 128:(kc + 1) * 128], ckb[:, kc, :], C.identb[:])
            return ins
        P.add('pe', tr, reads=['ckb', 'identb'], writes=[f'ps{pb}'])
        P.add('dve', lambda e: e.tensor_copy(C.ckT[:], psb[:, 0:512]), reads=[f'ps{pb}'], writes=['ckT'])


def _load_gate(C, l, s, w):
    P = C.P
    P.dma('sp', C.gate[:, :], C.modD[l, w, (3 * s + 2) * D:(3 * s + 3) * D].partition_broadcast(128),
          reads=[f'modD{l}'], writes=['gate'], nonc=True)


def _norm_p1(C, xin_ap, ss_col, rs_col, xn_ap, tags, gb):
    P = C.P
    xin_res, xn_res, hT_res = tags
    P.add('act', lambda e: e.activation(xn_ap, xin_ap, AF.Square, accum_out=ss_col), reads=[xin_res, f'ss{gb}'], writes=[xn_res, f'ss{gb}'])
    P.add('act', lambda e: e.activation(rs_col, ss_col, AF.Sqrt, bias=C.epsc[:, 0:1], scale=1.0 / D), reads=[f'ss{gb}', 'epsc'], writes=[f'rs{gb}'])
    P.add('dve', lambda e: e.reciprocal(rs_col, rs_col), reads=[f'rs{gb}'], writes=[f'rs{gb}'])
    P.add('dve', lambda e: e.tensor_scalar(xn_ap, xin_ap, rs_col, None, ALU.mult), reads=[xin_res, f'rs{gb}', xn_res], writes=[xn_res])


def _norm_p2(C, xn_ap, hT_dst, w, s, tags, gb):
    P = C.P
    xin_res, xn_res, hT_res = tags
    pb = _psn(C)
    psb = C.ps[pb][:].bitcast(BF16)

    def tr(e):
        ins = None
        for fc in range(8):
            ins = e.transpose(psb[:, fc * 128:(fc + 1) * 128], xn_ap[:, fc * 128:(fc + 1) * 128], C.identb[:])
        return ins
    P.add('pe', tr, reads=[xn_res, 'identb'], writes=[f'ps{pb}'])
    for fc in range(8):
        P.add('dve', lambda e, fc=fc: e.tensor_scalar(hT_dst(fc), psb[:, fc * 128:(fc + 1) * 128],
                                                      C.modA[:, w, s, fc:fc + 1], C.modB[:, w, s, fc:fc + 1], ALU.mult, ALU.add),
              reads=[f'ps{pb}', 'modA', 'modB'], writes=[f'{hT_res}{fc}'])


def _norm_block(C, xin_ap, ss_col, rs_col, xn_ap, hT_dst, w, s, tags, gb):
    _norm_p1(C, xin_ap, ss_col, rs_col, xn_ap, tags, gb)
    _norm_p2(C, xn_ap, hT_dst, w, s, tags, gb)


def _ffn_phase(C, l, f, s, first):
    nc, P = C.nc, C.P
    _load_gate(C, l, s, 0)
    tiles = []
    for t in range(4):
        tiles.append((C.xs if first else C.ys, C.ys, t * 1024, 8, 0))
    tiles.append((C.xp if first else C.yp, C.yp, 0, 4, 1))
    with (_sbt(nc, "xin", [128, 3, D], F32) as xin,
          _sbt(nc, "xres", [128, 3, D], F32) as xres,
          _sbt(nc, "xn", [128, 2, D], BF16) as xn,
          _sbt(nc, "hT", [128, 8, 1024], BF16) as hT,
          _sbt(nc, "actT", [128, NJ, 1024], BF16) as actT,
          _sbt(nc, "wi", [128, 6, 8, 256], BF16) as wi,
          _sbt(nc, "wo", [128, NJ, D], BF16) as wo,
          _sbt(nc, "sil", [128, 4, 512], F32) as sil,
          _sbt(nc, "tmp", [128, 2, D], F32) as tmp,
          _sbt(nc, "ssb", [128, 40], F32) as ssb,
          _sbt(nc, "rsb", [128, 40], F32) as rsb):
        P.add('dve', lambda e: e.memset(ssb[:], 0.0), writes=[f'ss{i}' for i in range(40)])
        nsil = 0
        nwi = 0
        nres = 0
        gstart = []
        g_ = 0
        for tl in tiles:
            gstart.append(g_)
            g_ += tl[3]

        def a0_ld(k, b):
            (src_, dst_, row0_, nb_, w_) = tiles[k]
            slot = (gstart[k] + b) % 3
            P.dma('sp', xin[:, slot, :], src_[row0_ + b * 128: row0_ + (b + 1) * 128, :], reads=[f'X{w_}_{row0_ + b * 128}'],
                  writes=[f'xin{slot}'])

        def a0_args(k, b):
            (src_, dst_, row0_, nb_, w_) = tiles[k]
            gb = gstart[k] + b
            slot = gb % 3
            xs_ = gb % 2
            return gb, slot, xs_, w_

        def a0_p1(k, b):
            gb, slot, xs_, w_ = a0_args(k, b)
            _norm_p1(C, xin[:, slot, :], ssb[:, gb:gb + 1], rsb[:, gb:gb + 1], xn[:, xs_, :], (f'xin{slot}', f'xn{xs_}', 'hT'), gb)

        def a0_p2(k, b):
            gb, slot, xs_, w_ = a0_args(k, b)
            _norm_p2(C, xn[:, xs_, :], lambda fc, b=b: hT[:, fc, b * 128:(b + 1) * 128], w_, s, (f'xin{slot}', f'xn{xs_}', 'hT'), gb)

        for b in range(2):
            a0_ld(0, b)
        for b in range(tiles[0][3]):
            if b + 2 < tiles[0][3]:
                a0_ld(0, b + 2)
            a0_p1(0, b)
            a0_p2(0, b)
        for ti, (src, dst, row0, nb, w) in enumerate(tiles):
            ntok = nb * 128
            if w == 1:
                _load_gate(C, l, s, 1)
            nxt = ti + 1 if ti + 1 < len(tiles) else None
            nnb = tiles[nxt][3] if nxt is not None else 0
            def ld_wi(j):
                slot = (nwi + j) % 6
                P.dma('sp', wi[:, slot], C.wbi[l, f, j], reads=[f'wbi{l}{f}_{j}'], writes=[f'wi{slot}'])
            for j in range(5):
                ld_wi(j)
            for h in range(2):
                P.dma('sp', wo[:, h * 11:(h + 1) * 11, :], C.wbo[l, f][:, h * 11:(h + 1) * 11, :],
                      reads=[f'wbo{l}{f}_{h}'], writes=['wo'])
            nh_n = ntok // 512
            for j in range(NJ):
                if j + 5 < NJ:
                    ld_wi(j + 5)
                slot = (nwi + j) % 6
                for nh in range(nh_n):
                    pg = _psn(C)
                    pu = _psn(C)
                    tok = slice(nh * 512, (nh + 1) * 512)
                    _mm_group(P, C.ps[pg][:, :], [(wi[:, slot, kc, 0:128], hT[:, kc, tok]) for kc in range(8)],
                              reads=[f'wi{slot}'] + [f'hT{k}' for k in range(8)], writes=[f'ps{pg}'])
                    _mm_group(P, C.ps[pu][:, :], [(wi[:, slot, kc, 128:256], hT[:, kc, tok]) for kc in range(8)],
                              reads=[f'wi{slot}'] + [f'hT{k}' for k in range(8)], writes=[f'ps{pu}'])
                    ss_ = nsil % 4
                    nsil += 1
                    P.add('act', lambda e, ss_=ss_, pg=pg: e.activation(sil[:, ss_, :], C.ps[pg][:, :], AF.Silu),
                          reads=[f'ps{pg}'], writes=[f'sil{ss_}'])
                    P.add('dve', lambda e, ss_=ss_, pu=pu, j=j, tok=tok: e.tensor_tensor(actT[:, j, tok], sil[:, ss_, :], C.ps[pu][:, :], ALU.mult),
                          reads=[f'sil{ss_}', f'ps{pu}'], writes=[f'actT{j}'])
            nwi += NJ
            def ld_res(b):
                slot = (nres + b) % 3
                P.dma('sp', xres[:, slot, :], src[row0 + b * 128: row0 + (b + 1) * 128, :], reads=[f'X{w}_{row0 + b * 128}'], writes=[f'xres{slot}'])
            for b in range(min(2, nb)):
                ld_res(b)
            if nxt is not None:
                a0_ld(nxt, 0)
                a0_ld(nxt, 1)
            for b in range(nb):
                if b + 2 < nb:
                    ld_res(b + 2)
                if nxt is not None and b < nnb:
                    a0_p1(nxt, b)
                slot = (nres + b) % 3
                ts_ = (nres + b) % 2
                for hf in range(2):
                    pb = _psn(C)
                    fs = slice(hf * 512, (hf + 1) * 512)
                    _mm_group(P, C.ps[pb][:, :], [(actT[:, jc, b * 128:(b + 1) * 128], wo[:, jc, fs]) for jc in range(NJ)],
                              reads=[f'actT{jc}' for jc in range(NJ)] + ['wo'], writes=[f'ps{pb}'])
                    P.add('dve', lambda e, pb=pb, fs=fs, ts_=ts_, w=w: e.scalar_tensor_tensor(
                        tmp[:, ts_, fs], C.ps[pb][:, :], 0.5, C.gate[:, fs], ALU.mult, ALU.mult),
                        reads=[f'ps{pb}', 'gate'], writes=[f'tmp{ts_}'])
                    P.add('dve', lambda e, fs=fs, ts_=ts_, slot=slot: e.tensor_tensor(
                        xres[:, slot, fs], tmp[:, ts_, fs], xres[:, slot, fs], ALU.add),
                        reads=[f'tmp{ts_}', f'xres{slot}'], writes=[f'xres{slot}'])
                P.dma('sp', dst[row0 + b * 128: row0 + (b + 1) * 128, :], xres[:, slot, :], reads=[f'xres{slot}'],
                      writes=[f'X{w}_{row0 + b * 128}'])
                if nxt is not None and b < nnb:
                    a0_p2(nxt, b)
                    if b + 2 < nnb:
                        a0_ld(nxt, b + 2)
            nres += nb


def _mixer_phase(C, l):
    nc, P = C.nc, C.P
    C.cur_l = l
    _load_gate(C, l, 1, 0)
    import contextlib
    with contextlib.ExitStack() as st:
        def sb(name, shape, dt=F32):
            return st.enter_context(_sbt(nc, name, list(shape), dt))
        wmi = sb("wmi", [128, 14, 8, 128], BF16)
        wmo = sb("wmo", [128, 8, D], BF16)
        xin = sb("m_xin", [128, 2, D])
        xres = sb("m_xres", [128, 2, D])
        xn = sb("m_xn", [128, 2, D], BF16)
        hT = sb("m_hT", [128, 8, 512], BF16)
        qT = sb("qT", [128, 2, 4, 512], BF16)
        kT = sb("kT", [128, T_S + T_P], BF16)
        vS = sb("vS", [128, 36, 128], BF16)
        uP = sb("uP", [128, 2, 2, 528])
        zc = sb("zc", [128, 2, 2, 514])
        bgT = sb("bgT", [128, 2, 2, 512], BF16)
        mixT = sb("mixT", [128, 8, 512], BF16)
        rope = sb("rope", [128, 2, 512])
        invc = sb("invc", [128, 2, 512])
        eT = sb("eT", [128, NET, 512], BF16)
        t1r = sb("t1", [128, 2, 512])
        t2r = sb("t2", [128, 2, 512])
        rsqr = sb("rsq", [128, 2, 512])
        sqbr = sb("sqb", [128, 2, 512], BF16)
        qbr = sb("qb", [128, 2, 512], BF16)
        ucv = sb("ucv", [128, 512])
        khl = sb("khl", [128, 2, 512], BF16)
        pA = sb("pA", [128, 2, 528])
        pB = sb("pB", [128, 2, 528])
        plb = sb("plb", [128, 2, 512], BF16)
        acc = sb("acc", [128, 2, 512])
        kout = acc[:, 0, :].rearrange("p (b f) -> p b f", b=4)
        vout = acc[:, 1, :].rearrange("p (b f) -> p b f", b=4)
        tmp = sb("m_tmp", [128, 2, 512])
        dsum = sb("dsum", [128, 512])
        ssb = sb("m_ssb", [128, 40])
        rsb = sb("m_rsb", [128, 40])

        P.dma('sp', wmi[:], C.wbm[l], reads=[f'wbm{l}'], writes=['wmi'])
        P.dma('sp', wmo[:], C.wbw[l], reads=[f'wbw{l}'], writes=['wmo'])
        P.add('dve', lambda e: e.memset(ssb[:], 0.0), writes=[f'ss{i}' for i in range(40)])
        st_ = dict(nblk=0, ne=0, nres=0, invk=-1)

        def n_tile(k):
            return (C.yp, 0, 1) if k == 8 else (C.ys, k * 512, 0)

        def n_ld(k, b):
            src, row0, w = n_tile(k)
            slot = (k * 4 + b) % 2
            P.dma('sp', xin[:, slot, :], src[row0 + b * 128: row0 + (b + 1) * 128, :],
                  reads=[f'X{w}_{row0 + b * 128}'], writes=[f'xin{slot}'])

        def n_p1(k, b):
            src, row0, w = n_tile(k)
            gb = k * 4 + b
            slot = gb % 2
            _norm_p1(C, xin[:, slot, :], ssb[:, gb:gb + 1], rsb[:, gb:gb + 1], xn[:, slot, :], (f'xin{slot}', f'xn{slot}', 'hT'), gb)

        def n_p2(k, b):
            src, row0, w = n_tile(k)
            gb = k * 4 + b
            slot = gb % 2
            _norm_p2(C, xn[:, slot, :], lambda fc, b=b: hT[:, fc, b * 128:(b + 1) * 128], w, 1, (f'xin{slot}', f'xn{slot}', 'hT'), gb)

        def m0(k):
            n_ld(k, 0)
            for b in range(4):
                if b + 1 < 4:
                    n_ld(k, b + 1)
                n_p1(k, b)
                n_p2(k, b)

        def rope_ld(rc):
            P.dma('sp', rope[:, 0, :], C.c_rope[0][:, rc:rc + 512], writes=['rope'])
            P.dma('sp', rope[:, 1, :], C.c_rope[1][:, rc:rc + 512], writes=['rope'])

        def proj(co):
            pb = _psn(C)
            _mm_group(P, C.ps[pb][:, :], [(wmi[:, co // 128, kc, :], hT[:, kc, :]) for kc in range(8)],
                      reads=['wmi'] + [f'hT{k}' for k in range(8)], writes=[f'ps{pb}'])
            return pb

        def qk_evac(pq, gcol, RT, dst_fn, rope_col, dst_res, f32_dst=None):
            ps = C.ps[pq]
            C.qcall = getattr(C, 'qcall', -1) + 1
            rr = C.qcall % 2
            t1, t2, rsq, sqb, qb = t1r[:, rr], t2r[:, rr], rsqr[:, rr], sqbr[:, rr], qbr[:, rr]
            n_t1, n_t2, n_rsq, n_sqb, n_qb = f't1_{rr}', f't2_{rr}', f'rsq_{rr}', f'sqb_{rr}', f'qb_{rr}'
            P.add('act', lambda e: e.activation(sqb, ps[:, :], AF.Square), reads=[f'ps{pq}'], writes=[n_sqb])
            P.add('act', lambda e: e.copy(qb, ps[:, :]), reads=[f'ps{pq}'], writes=[n_qb])
            p1 = _psn(C)
            p2 = _psn(C)
            _mm_group(P, C.ps[p1][:, :], [(C.bonesb[:], sqb)], reads=['bonesb', n_sqb], writes=[f'ps{p1}'])
            _mm_group(P, C.ps[p2][:, :], [(RT[:], qb)], reads=['RqT', 'RkT', n_qb], writes=[f'ps{p2}'])
            P.add('dve', lambda e: e.scalar_tensor_tensor(t1, ps[:, :], C.gqk[:, gcol:gcol + 1], rope[:, 0, :], ALU.mult, ALU.mult),
                  reads=[f'ps{pq}', 'gqk', 'rope'], writes=[n_t1])
            P.add('dve', lambda e: e.tensor_tensor(t2, C.ps[p2][:, :], rope[:, 1, :], ALU.mult),
                  reads=[f'ps{p2}', 'rope'], writes=[n_t2])
            P.add('pool', lambda e: e.tensor_tensor(t1, t1, t2, ALU.add), reads=[n_t1, n_t2], writes=[n_t1])
            P.add('act', lambda e: e.activation(rsq, C.ps[p1][:, :], AF.Sqrt, bias=C.epsc[:, 0:1], scale=1.0 / 64),
                  reads=[f'ps{p1}', 'epsc'], writes=[n_rsq])
            P.add('dve', lambda e: e.reciprocal(rsq, rsq), reads=[n_rsq], writes=[n_rsq])
            if f32_dst is None:
                P.add('pool', lambda e: e.tensor_tensor(dst_fn(), t1, rsq, ALU.mult), reads=[n_t1, n_rsq], writes=[dst_res])
            else:
                P.add('dve', lambda e: e.tensor_tensor(f32_dst, t1, rsq, ALU.mult), reads=[n_t1, n_rsq], writes=['ucv'])
                P.add('pool', lambda e: e.tensor_copy(dst_fn(), f32_dst), reads=['ucv'], writes=[dst_res])

        def m1(tok0, slot, w, prompt, l=l):
            rc = T_S if prompt else tok0
            for c in range(4):
                pq = proj(c * 128)
                qk_evac(pq, 0, C.RqT, lambda c=c: qT[:, slot, c, :], rc, f'qT{slot}_{c}')
            pk = proj(512)
            if prompt:
                kf = ucv
                qk_evac(pk, 1, C.RkT, lambda: kT[:, tok0:tok0 + 512], rc, f'kT{tok0 // 512}', f32_dst=kf[:])
                P.add('act', lambda e: e.copy(khl[:, 0, :], kf[:]), reads=['ucv'], writes=['khi'])
                P.add('dve', lambda e: e.tensor_tensor(khl[:, 1, :], kf[:], khl[:, 0, :], ALU.subtract), reads=['ucv', 'khi'], writes=['klo'])
                pb = _psn(C)
                psb = C.ps[pb][:].bitcast(BF16)

                def trk(e):
                    ins = None
                    for hl in range(2):
                        for b in range(4):
                            ins = e.transpose(psb[:, hl * 512 + b * 128: hl * 512 + (b + 1) * 128], khl[:, hl, b * 128:(b + 1) * 128],
                                              C.identb[:])
                    return ins
                P.add('pe', trk, reads=['khi', 'klo', 'identb'], writes=[f'ps{pb}'])
                P.add('act', lambda e: e.copy(kout, psb[:, 0:512].rearrange("p (b f) -> p b f", b=4)),
                      reads=[f'ps{pb}'], writes=['acc0'])
                P.add('dve', lambda e: e.tensor_tensor(kout, kout, psb[:, 512:1024].rearrange("p (b f) -> p b f", b=4), ALU.add),
                      reads=[f'ps{pb}', 'acc0'], writes=['acc0'])
                for pi in range(2):
                    P.dma('sp', C.nk[pi, l].rearrange("(b p) f -> p b f", p=128), kout[:, pi * 2:(pi + 1) * 2, :],
                          reads=['acc0'], writes=[f'nk{pi}{l}'])
            else:
                qk_evac(pk, 1, C.RkT, lambda: kT[:, tok0:tok0 + 512], rc, f'kT{tok0 // 512}')
            pb = _psn(C)
            for b in range(4):
                _mm_group(P, C.ps[pb][:, b * 128:(b + 1) * 128],
                          [(hT[:, kc, b * 128:(b + 1) * 128], wmi[:, 5, kc, :]) for kc in range(8)],
                          reads=['wmi'] + [f'hT{k}' for k in range(8)], writes=[f'ps{pb}'])
            blk0 = tok0 // 128
            P.add('act', lambda e: e.copy(vS[:, blk0:blk0 + 4, :], C.ps[pb][:, :].rearrange("p (b f) -> p b f", b=4)),
                  reads=[f'ps{pb}'], writes=[f'vS{tok0 // 512}'])
            if prompt:
                P.add('dve', lambda e: e.tensor_copy(vout, C.ps[pb][:, :].rearrange("p (b f) -> p b f", b=4)),
                      reads=[f'ps{pb}'], writes=['acc1'])
                for pi in range(2):
                    P.dma('sp', C.nv[pi, l].rearrange("(b p) f -> p b f", p=128), vout[:, pi * 2:(pi + 1) * 2, :],
                          reads=['acc1'], writes=[f'nv{pi}{l}'])
            for c in range(2):
                pu = proj(768 + c * 128)
                if prompt:
                    for i in range(2):
                        P.add('act', lambda e, c=c, i=i, pu=pu: e.copy(uP[:, i, c, 8:264], C.ps[pu][:, i * 256:(i + 1) * 256]),
                              reads=[f'ps{pu}'], writes=[f'uP{i}'])
                else:
                    P.add('act', lambda e, c=c, pu=pu: e.copy(uP[:, slot, c, 8:520], C.ps[pu][:, :]),
                          reads=[f'ps{pu}'], writes=[f'uP{slot}'])
            for c in range(2):
                pu = proj(1024 + c * 128)
                P.add('act', lambda e, pu=pu: e.copy(ucv[:], C.ps[pu][:, :]), reads=[f'ps{pu}'], writes=['ucv'])
                pc = proj(1536 + c * 128)
                if prompt:
                    for i in range(2):
                        P.add('dve', lambda e, c=c, i=i, pc=pc: e.tensor_tensor(zc[:, i, c, 1:257], C.ps[pc][:, i * 256:(i + 1) * 256],
                                                                                 ucv[:, i * 256:(i + 1) * 256], ALU.mult),
                              reads=[f'ps{pc}', 'ucv'], writes=[f'zc{i}'])
                else:
                    P.add('dve', lambda e, c=c, pc=pc: e.tensor_tensor(zc[:, slot, c, 1:513], C.ps[pc][:, :], ucv[:], ALU.mult),
                          reads=[f'ps{pc}', 'ucv'], writes=[f'zc{slot}'])
            for c in range(2):
                pg = proj(1280 + c * 128)
                P.add('act', lambda e, c=c, pg=pg: e.copy(bgT[:, slot, c, :], C.ps[pg][:, :]), reads=[f'ps{pg}'], writes=[f'bgT{slot}'])

        def attention(slot, qcol_blocks, nxt=None, g_list=(0, 1)):
            pend = None
            if nxt is not None:
                n_ld(nxt, 0)
                n_ld(nxt, 1)
            for (b, chunks) in qcol_blocks:
                for g in g_list:
                    if nxt is not None and g == 0:
                        n_p1(nxt, b)
                        if b + 2 < 4:
                            n_ld(nxt, b + 2)
                    cur = att_a(slot, b, g, chunks)
                    if pend is not None:
                        att_b(*pend)
                    pend = cur
                    if nxt is not None and g == 1:
                        n_p2(nxt, b)
            att_b(*pend)

        def att_a(slot, b, g, chunks):
            gs = slice(g * 64, (g + 1) * 64)
            rhs_q = qT[gs, slot, :, b * 128:(b + 1) * 128]
            ech = []
            for (kfn, vap, mi, rds) in chunks:
                pb = _psn(C)
                out = C.ps[pb][:, :].rearrange("p (c t) -> p c t", c=4)

                def smm(e, out=out, kap=kfn(gs), rq=rhs_q, mi=mi):
                    ins = e.matmul(out, kap, rq, start=True, stop=(mi is None))
                    if mi is not None:
                        ins = e.matmul(out, C.identb[:], C.maskb[:, mi, :].rearrange("p (c t) -> p c t", c=4),
                                       start=False, stop=True)
                    return ins
                P.add('pe', smm, reads=rds + [f'qT{slot}_{c}' for c in range(4)] + ['identb', 'maskb'],
                      writes=[f'ps{pb}'])
                es = st_['ne'] % NET
                st_['ne'] += 1
                P.add('act', lambda e, es=es, pb=pb: e.activation(eT[:, es, :], C.ps[pb][:, :], AF.Exp, scale=0.125),
                      reads=[f'ps{pb}'], writes=[f'eT{es}'])
                ech.append((es, vap, rds))
            return (b, g, ech)

        def att_b(b, g, ech):
            gs = slice(g * 64, (g + 1) * 64)
            po = _psn(C)
            pd = _psn(C)
            _mm_group(P, C.ps[po][:, :], [(vap, eT[:, es, :]) for (es, vap, rds) in ech],
                      reads=[f'eT{es}' for (es, _, _) in ech] + sum([r for (_, _, r) in ech], []) + ['cvS'],
                      writes=[f'ps{po}'])
            _mm_group(P, C.ps[pd][:, :], [(C.onesb[:], eT[:, es, :]) for (es, vap, rds) in ech],
                      reads=[f'eT{es}' for (es, _, _) in ech] + ['onesb'], writes=[f'ps{pd}'])
            dg = st_['ne'] % 2
            P.add('dve', lambda e: e.tensor_tensor(dsum[gs, :], C.ps[pd][gs, :], C.sinkexp[gs, g * 512:(g + 1) * 512], ALU.add),
                  reads=[f'ps{pd}', 'sinkexp'], writes=[f'dsum{g}'])
            P.add('dve', lambda e: e.reciprocal(dsum[gs, :], dsum[gs, :]), reads=[f'dsum{g}'], writes=[f'dsum{g}'])
            P.add('dve', lambda e: e.tensor_tensor(
                mixT[gs, 0:4, b * 128:(b + 1) * 128], C.ps[po][gs, :].rearrange("p (c t) -> p c t", c=4),
                dsum[gs, :].rearrange("p (c t) -> p c t", c=4), ALU.mult),
                reads=[f'ps{po}', f'dsum{g}'], writes=[f'mixA{g}'])

        def poolconv2(n, mcol):
            for c in range(2):
                pb = _psn(C)
                _mm_group(P, C.ps[pb][:, 0:n], [(C.pwbd[:, c, :], plb[:, c, 0:n])], reads=['pwbd', 'plb'], writes=[f'ps{pb}'])
                P.add('dve', lambda e, c=c, pb=pb: e.tensor_scalar(mixT[:, 4 + c, mcol:mcol + n], C.ps[pb][:, 0:n],
                                                                   C.psc[:, c:c + 1], None, ALU.mult),
                      reads=[f'ps{pb}', 'psc'], writes=[f'mixP{c}'])

        def poolconv(uslot, n, mcol, kind, bg_ap_fn, stage=0):
            if stage == 2:
                return poolconv2(n, mcol)
            if st_['invk'] != kind:
                P.dma('sp', invc[:], C.c_invc[kind], writes=['invc'])
                st_['invk'] = kind
            icol = mcol if kind == 3 else 0
            U = uP[:, uslot]
            ur = f'uP{uslot}'
            P.add('pool', lambda e: e.tensor_tensor(pA[:, :, 1:n + 16], U[:, :, 0:n + 15], U[:, :, 1:n + 16], ALU.add), reads=[ur], writes=['pA'])
            P.add('pool', lambda e: e.tensor_tensor(pB[:, :, 2:n + 15], pA[:, :, 1:n + 14], pA[:, :, 3:n + 16], ALU.add), reads=['pA'], writes=['pB'])
            P.add('pool', lambda e: e.tensor_tensor(pA[:, 1, 4:n + 13], pB[:, 1, 2:n + 11], pB[:, 1, 6:n + 15], ALU.add), reads=['pB', 'pA'], writes=['pA'])
            P.add('pool', lambda e: e.tensor_tensor(pB[64:128, 1, 8:n + 8], pA[64:128, 1, 4:n + 4], pA[64:128, 1, 12:n + 12], ALU.add),
                  reads=['pA', 'pB'], writes=['pB'])
            P.add('pool', lambda e: e.tensor_copy(pA[64:128, :, 8:8 + n], pB[64:128, :, 8:8 + n]), reads=['pB', 'pA'], writes=['pA'])
            P.add('pool', lambda e: e.tensor_tensor(pA[:, :, 8:8 + n], pA[:, :, 8:8 + n], invc[:, :, icol:icol + n], ALU.mult),
                  reads=['pA', 'invc'], writes=['pA'])
            P.add('pool', lambda e: e.tensor_tensor(plb[:, :, 0:n], pA[:, :, 8:8 + n], U[:, :, 8:8 + n], ALU.subtract),
                  reads=['pA', ur], writes=['plb'])
            if stage == 0:
                poolconv2(n, mcol)
            Z = zc[:, uslot]
            zr = f'zc{uslot}'
            for c in range(2):
                P.add('dve', lambda e, c=c: e.tensor_scalar(acc[:, c, 0:n], Z[:, c, 0:n], C.cw[:, c, 0:1], None, ALU.mult),
                      reads=[zr, 'cw'], writes=[f'acc{c}'])
                P.add('dve', lambda e, c=c: e.scalar_tensor_tensor(acc[:, c, 0:n], Z[:, c, 1:n + 1], C.cw[:, c, 1:2], acc[:, c, 0:n],
                                                                     ALU.mult, ALU.add), reads=[zr, 'cw', f'acc{c}'], writes=[f'acc{c}'])
                P.add('dve', lambda e, c=c: e.scalar_tensor_tensor(acc[:, c, 0:n], Z[:, c, 2:n + 2], C.cw[:, c, 2:3], acc[:, c, 0:n],
                                                                     ALU.mult, ALU.add), reads=[zr, 'cw', f'acc{c}'], writes=[f'acc{c}'])
                P.add('pool', lambda e, c=c: e.tensor_tensor(mixT[:, 6 + c, mcol:mcol + n], acc[:, c, 0:n], bg_ap_fn(c), ALU.mult),
                      reads=[f'acc{c}', 'bgT0', 'bgT1'], writes=[f'mixC{c}'])

        def wout_stage(dst_src, row0, w):
            def ld(b):
                slot = (st_['nres'] + b) % 2
                P.dma('sp', xres[:, slot, :], dst_src[0][row0 + b * 128: row0 + (b + 1) * 128, :],
                      reads=[f'X{w}_{row0 + b * 128}'], writes=[f'xres{slot}'])
            ld(0)
            mix_reads = ['mixA0', 'mixA1', 'mixP0', 'mixP1', 'mixC0', 'mixC1', 'wmo']
            for b in range(4):
                if b + 1 < 4:
                    ld(b + 1)
                slot = (st_['nres'] + b) % 2
                for hf in range(2):
                    pb = _psn(C)
                    fs = slice(hf * 512, (hf + 1) * 512)
                    _mm_group(P, C.ps[pb][:, :], [(mixT[:, c, b * 128:(b + 1) * 128], wmo[:, c, fs]) for c in range(8)],
                              reads=mix_reads, writes=[f'ps{pb}'])
                    P.add('dve', lambda e, pb=pb, fs=fs, hf=hf: e.tensor_tensor(tmp[:, hf, :], C.ps[pb][:, :], C.gate[:, fs], ALU.mult),
                          reads=[f'ps{pb}', 'gate'], writes=[f'tmpm{hf}'])
                    P.add('pool', lambda e, fs=fs, slot=slot, hf=hf: e.tensor_tensor(xres[:, slot, fs], tmp[:, hf, :], xres[:, slot, fs], ALU.add),
                          reads=[f'tmpm{hf}', f'xres{slot}'], writes=[f'xres{slot}'])
                P.dma('sp', dst_src[1][row0 + b * 128: row0 + (b + 1) * 128, :], xres[:, slot, :], reads=[f'xres{slot}'],
                      writes=[f'X{w}_{row0 + b * 128}'])
            st_['nres'] += 4

        def m2_sample(t, nxt=None):
            slot = t % 2
            blocks = []
            for b in range(4):
                gb = 4 * t + b
                chunks = []
                for kc in range(4):
                    chunks.append((lambda gs, kc=kc: C.ckT[gs, kc * 128:(kc + 1) * 128], C.cvS[:, kc, :], None, ['ckT']))
                for (blk, mi) in ((gb - 1, 0), (gb, None), (gb + 1, 1)):
                    if 0 <= blk < 32:
                        chunks.append((lambda gs, blk=blk: kT[gs, blk * 128:(blk + 1) * 128], vS[:, blk, :], mi,
                                       [f'kT{blk // 4}', f'vS{blk // 4}']))
                blocks.append((b, chunks))
            kind = 0 if t == 0 else (2 if t == 7 else 1)
            poolconv(slot, 512, 0, kind, lambda c: bgT[:, slot, c, :], stage=1)
            attention(slot, blocks, nxt)
            poolconv(slot, 512, 0, kind, lambda c: bgT[:, slot, c, :], stage=2)
            wout_stage((C.ys, C.ys), t * 512, 0)

        try:
            _mixer_main(C, m0, m1, m2_sample, attention, poolconv, wout_stage, uP, zc, kT, vS, bgT, rope_ld)
        except _Stop:
            pass


def _mixer_main(C, m0, m1, m2_sample, attention, poolconv, wout_stage, uP, zc, kT, vS, bgT, rope_ld):
    P = C.P
    if True:
        rope_ld(0)
        m0(0)
        for t in range(8):
            s = t % 2
            m1(t * 512, s, 0, False)
            rope_ld((t + 1) * 512 if t < 7 else T_S)
            if t == 0:
                P.add('pool', lambda e: e.memset(uP[:, 0, :, 0:8], 0.0), reads=['uP0'], writes=['uP0'])
                P.add('pool', lambda e: e.memset(zc[:, 0, :, 0:1], 0.0), reads=['zc0'], writes=['zc0'])
            else:
                o = 1 - s
                P.add('pool', lambda e, s=s, o=o: e.tensor_copy(uP[:, s, :, 0:8], uP[:, o, :, 512:520]), reads=[f'uP{o}', f'uP{s}'], writes=[f'uP{s}'])
                P.add('pool', lambda e, s=s, o=o: e.tensor_copy(uP[:, o, :, 520:528], uP[:, s, :, 8:16]), reads=[f'uP{o}', f'uP{s}'], writes=[f'uP{o}'])
                P.add('pool', lambda e, s=s, o=o: e.tensor_copy(zc[:, s, :, 0:1], zc[:, o, :, 512:513]), reads=[f'zc{o}', f'zc{s}'], writes=[f'zc{s}'])
                P.add('pool', lambda e, s=s, o=o: e.tensor_copy(zc[:, o, :, 513:514], zc[:, s, :, 1:2]), reads=[f'zc{o}', f'zc{s}'], writes=[f'zc{o}'])
            if t == 7:
                P.add('pool', lambda e, s=s: e.memset(uP[:, s, :, 520:528], 0.0), reads=[f'uP{s}'], writes=[f'uP{s}'])
                P.add('pool', lambda e, s=s: e.memset(zc[:, s, :, 513:514], 0.0), reads=[f'zc{s}'], writes=[f'zc{s}'])
            if t >= 1:
                m2_sample(t - 1, t + 1)
            else:
                m0(1)
        m2_sample(7, None)
        _load_gate(C, C.cur_l, 1, 1)
        for i in range(2):
            P.add('pool', lambda e, i=i: e.memset(uP[:, i, :, :], 0.0), reads=[f'uP{i}'], writes=[f'uP{i}'])
            P.add('pool', lambda e, i=i: e.memset(zc[:, i, :, :], 0.0), reads=[f'zc{i}'], writes=[f'zc{i}'])
        m1(T_S, 0, 1, True)
        blocks = []
        for b in range(4):
            sq = b // 2
            chunks = []
            for kb in range(2):
                blk = 32 + sq * 2 + kb
                chunks.append((lambda gs, blk=blk: kT[gs, blk * 128:(blk + 1) * 128], vS[:, blk, :], None, ['kT8', 'vS8']))
            blocks.append((b, chunks))
        attention(0, blocks)
        for i in range(2):
            poolconv(i, 256, i * 256, 3, lambda c, i=i: bgT[:, 0, c, i * 256:(i + 1) * 256])
        wout_stage((C.yp, C.yp), 0, 1)


def kernel(**inputs):
    f = lambda k: np.ascontiguousarray(np.asarray(inputs[k], dtype=np.float32))
    x_prompt, x_sample = f('x_prompt'), f('x_sample')
    cache_k, cache_v, c, c_ctx = f('cache_k'), f('cache_v'), f('c'), f('c_ctx')
    shared = {k: f(k) for k in ('w_ada', 'b_ada', 'norm_g', 'w_ffn_in', 'w_ffn_out', 'w_in', 'w_out', 'q_norm_g',
                                'k_norm_g', 'sink', 'pool_w', 'pool_scale', 'conv_w')}
    shared.update(_host_constants())
    nc = build_program()
    in_maps = []
    for i in range(8):
        m = dict(shared)
        m['xs'] = x_sample[i]
        m['xp'] = np.ascontiguousarray(x_prompt[2 * i:2 * i + 2].reshape(T_P, D))
        m['ck'] = np.ascontiguousarray(cache_k[i].reshape(NL, 512, 128))
        m['cv'] = np.ascontiguousarray(cache_v[i].reshape(NL, 512, 128))
        m['cond'] = np.ascontiguousarray(np.stack([c[i], c_ctx]))
        in_maps.append(m)
    res = run_bass_kernel_spmd(nc, in_maps, core_ids=list(range(8)))
    r = res.results
    ys = np.stack([np.asarray(r[i]['ys'], np.float32) for i in range(8)])
    yp = np.concatenate([np.asarray(r[i]['yp'], np.float32).reshape(2, 256, D) for i in range(8)], axis=0)
    nk = np.concatenate([np.asarray(r[i]['nk'], np.float32) for i in range(8)], axis=0).reshape(16, NL, 256, 2, 64)
    nv = np.concatenate([np.asarray(r[i]['nv'], np.float32) for i in range(8)], axis=0).reshape(16, NL, 256, 2, 64)
    return (yp, ys, nk, nv)
```
